# Optimizing a Trainium2 kernel written in Bass

```python
import math
import jax, jax.numpy as jnp
from jax import lax
import numpy as np

D_MODEL = 1024
BATCH = 8
SEQ = 2048
DEPTH = 4

N_MIXERS = 3
EPS = 1e-6
D_FF = -(-8 * D_MODEL // (3 * 256)) * 256
FOURIER_GROUP = 128
N_FOURIER_GROUPS = D_MODEL // FOURIER_GROUP
S5_GROUP = 16
S5_GROUPS = D_MODEL // S5_GROUP
S5_STATE = 64
S5_DT_MIN = 1e-3
S5_DT_MAX = 1e-1
HEAD_DIM = 64
HEADS_PER_GROUP = D_MODEL // HEAD_DIM
DILATED_GROUPS = ((128, 1), (512, 4), (2048, 16))
N_ATTN_GROUPS = len(DILATED_GROUPS)
N_ATTN_HEADS = N_ATTN_GROUPS * HEADS_PER_GROUP
QKV_WIDTH = N_ATTN_GROUPS * 3 * HEADS_PER_GROUP * HEAD_DIM
ATTN_OUT = HEADS_PER_GROUP * HEAD_DIM
NUM_BUCKETS = 32
MAX_DISTANCE = 1024
N_LAYERS_A = (DEPTH + 2) // 3
N_LAYERS_B = (DEPTH + 1) // 3
N_LAYERS_C = DEPTH // 3

kernel_name = "hybrid_fnet_s5_dilated_encoder"


def rms_norm(x, g):
    xf = x.astype(jnp.float32)
    y = xf * lax.rsqrt(jnp.mean(xf * xf, axis=-1, keepdims=True) + EPS)
    return (y * g.astype(jnp.float32)).astype(x.dtype)


def swiglu(h, w_gate_up, w_down):
    gate, up = jnp.split(h @ w_gate_up, 2, axis=-1)
    return (jax.nn.silu(gate) * up) @ w_down


def fourier_mixer(h, w_out):
    b, s, d = h.shape
    hg = h.astype(jnp.float32).reshape(b, s, N_FOURIER_GROUPS, FOURIER_GROUP)
    f = jnp.fft.fft2(hg, axes=(1, 3), norm="ortho").real
    return f.reshape(b, s, d).astype(h.dtype) @ w_out


def _s5_direction(u, lam_re, lam_im, log_dt, b_re, b_im, c_re, c_im, reverse):
    f32 = jnp.float32
    lam = lax.complex(lam_re.astype(f32), lam_im.astype(f32))
    dt = jnp.exp(log_dt.astype(f32))[:, None]
    lam_bar = jnp.exp(lam * dt)
    b_bar = ((lam_bar - 1.0) / lam)[..., None] * lax.complex(b_re.astype(f32), b_im.astype(f32))
    bu = lax.complex(jnp.einsum('bsgp,gnp->bsgn', u, b_bar.real),
                     jnp.einsum('bsgp,gnp->bsgn', u, b_bar.imag))
    a = jnp.broadcast_to(lam_bar, bu.shape)

    def combine(e1, e2):
        a1, x1 = e1
        a2, x2 = e2
        return a1 * a2, a2 * x1 + x2

    _, hs = lax.associative_scan(combine, (a, bu), axis=1, reverse=reverse)
    c = lax.complex(c_re.astype(f32), c_im.astype(f32))
    return jnp.einsum('gpn,bsgn->bsgp', c, hs).real


def s5_mixer(h, lam_re, lam_im, log_dt, b_re, b_im, c_re, c_im, d_skip, w_glu):
    b, s, d = h.shape
    hf = h.astype(jnp.float32)
    u = hf.reshape(b, s, S5_GROUPS, S5_GROUP)
    y_fwd = _s5_direction(u, lam_re[0], lam_im[0], log_dt[0], b_re[0], b_im[0], c_re[0], c_im[0], False)
    y_bwd = _s5_direction(u, lam_re[1], lam_im[1], log_dt[1], b_re[1], b_im[1], c_re[1], c_im[1], True)
    y = (y_fwd + y_bwd).reshape(b, s, d) + d_skip.astype(jnp.float32) * hf
    g = jax.nn.gelu(y).astype(h.dtype)
    val, gate = jnp.split(g @ w_glu, 2, axis=-1)
    return val * jax.nn.sigmoid(gate)


def t5_bucket(rel):
    half = NUM_BUCKETS // 2
    max_exact = half // 2
    n = np.abs(rel)
    sign = (rel > 0).astype(np.int32) * half
    large = max_exact + (np.log(np.maximum(n, 1) / max_exact) / math.log(MAX_DISTANCE / max_exact)
                         * (half - max_exact)).astype(np.int32)
    large = np.minimum(large, half - 1)
    return (sign + np.where(n < max_exact, n, large)).astype(np.int32)


def head_rms_norm(t, g):
    tf = t.astype(jnp.float32)
    y = tf * lax.rsqrt(jnp.mean(tf * tf, axis=-1, keepdims=True) + EPS)
    return (y * g.astype(jnp.float32)).astype(t.dtype)


def dilated_group_attention(q, k, v, bias_g, dil, n_side):
    bsz, s, nh, e = q.shape
    seg = s // dil
    blk = n_side
    nb = -(-seg // blk)
    segp = nb * blk

    def res(t):
        return t.reshape(bsz, seg, dil, nh, e).transpose(0, 2, 1, 3, 4)

    qb = jnp.pad(res(q), ((0, 0), (0, 0), (0, segp - seg), (0, 0), (0, 0))).reshape(bsz, dil, nb, blk, nh, e)

    def kv_blocks(t):
        tp = jnp.pad(res(t), ((0, 0), (0, 0), (blk, segp - seg + blk), (0, 0), (0, 0)))
        tp = tp.reshape(bsz, dil, nb + 2, blk, nh, e)
        return jnp.concatenate([tp[:, :, :-2], tp[:, :, 1:-1], tp[:, :, 2:]], axis=3)

    kb = kv_blocks(k)
    vb = kv_blocks(v)
    qi = np.arange(blk)[:, None]
    ki = np.arange(3 * blk)[None, :]
    off = ki - blk - qi
    band = np.abs(off) <= n_side
    key_idx = np.arange(nb)[:, None] * blk + np.arange(3 * blk)[None, :] - blk
    in_range = (key_idx >= 0) & (key_idx < seg)
    allowed = band[None] & in_range[:, None, :]
    buckets = t5_bucket(off * dil)
    bias = jnp.transpose(bias_g.astype(jnp.float32)[buckets], (2, 0, 1))

    sc = jnp.einsum('brnqhe,brnkhe->brnhqk', qb, kb, preferred_element_type=jnp.float32) * (e ** -0.5)
    sc = jnp.where(allowed[None, None, :, None], sc + bias, -1e30)
    m = jnp.max(sc, axis=-1, keepdims=True)
    p = jnp.exp(sc - m)
    den = jnp.sum(p, axis=-1, keepdims=True)
    o = jnp.einsum('brnhqk,brnkhe->brnqhe', (p / den).astype(v.dtype), vb)
    lse = (m + jnp.log(den))[..., 0]
    o = o.reshape(bsz, dil, segp, nh, e)[:, :, :seg].transpose(0, 2, 1, 3, 4).reshape(bsz, s, nh, e)
    lse = lse.transpose(0, 1, 2, 4, 3).reshape(bsz, dil, segp, nh)[:, :, :seg]
    lse = lse.transpose(0, 2, 1, 3).reshape(bsz, s, nh)
    return o, lse


def dilated_attention_mixer(h, w_qkv, q_gain, k_gain, w_o, rel_bias):
    b, s, d = h.shape
    qkv = (h @ w_qkv).reshape(b, s, N_ATTN_GROUPS, 3, HEADS_PER_GROUP, HEAD_DIM)
    outs = []
    lses = []
    for g, (window, dil) in enumerate(DILATED_GROUPS):
        q = head_rms_norm(qkv[:, :, g, 0], q_gain[g])
        k = head_rms_norm(qkv[:, :, g, 1], k_gain[g])
        v = qkv[:, :, g, 2]
        bias_g = rel_bias[:, g * HEADS_PER_GROUP:(g + 1) * HEADS_PER_GROUP]
        o, l = dilated_group_attention(q, k, v, bias_g, dil, (window // 2) // dil)
        outs.append(o)
        lses.append(l)
    wts = jax.nn.softmax(jnp.stack(lses, axis=0), axis=0)
    o = jnp.einsum('gbsh,gbshe->bshe', wts, jnp.stack(outs, axis=0).astype(jnp.float32))
    return o.reshape(b, s, ATTN_OUT).astype(h.dtype) @ w_o


def setup_inputs(seed: int = 0) -> dict:
    key = jax.random.key(seed)
    ks = jax.random.split(key, 24)
    f32 = jnp.float32
    d = D_MODEL

    def nrm(k, shape, scale):
        return jax.random.normal(k, shape, f32) * scale

    s5_shape = (N_LAYERS_B, 2, S5_GROUPS, S5_STATE)
    n_idx = jnp.arange(S5_STATE, dtype=f32)
    return {
        "x": nrm(ks[0], (BATCH, SEQ, d), 1.0),
        "norm_mix_g": 1.0 + nrm(ks[1], (DEPTH, d), 0.1),
        "norm_ffn_g": 1.0 + nrm(ks[2], (DEPTH, d), 0.1),
        "fnet_w_out": nrm(ks[3], (N_LAYERS_A, d, d), d ** -0.5),
        "s5_lambda_re": -0.5 + nrm(ks[4], s5_shape, 0.01),
        "s5_lambda_im": math.pi * n_idx + nrm(ks[5], s5_shape, 0.01),
        "s5_log_dt": jax.random.uniform(ks[6], (N_LAYERS_B, 2, S5_GROUPS), f32,
                                        math.log(S5_DT_MIN), math.log(S5_DT_MAX)),
        "s5_b_re": nrm(ks[7], (N_LAYERS_B, 2, S5_GROUPS, S5_STATE, S5_GROUP), (2 * S5_GROUP) ** -0.5),
        "s5_b_im": nrm(ks[8], (N_LAYERS_B, 2, S5_GROUPS, S5_STATE, S5_GROUP), (2 * S5_GROUP) ** -0.5),
        "s5_c_re": nrm(ks[9], (N_LAYERS_B, 2, S5_GROUPS, S5_GROUP, S5_STATE), S5_STATE ** -0.5),
        "s5_c_im": nrm(ks[10], (N_LAYERS_B, 2, S5_GROUPS, S5_GROUP, S5_STATE), S5_STATE ** -0.5),
        "s5_d": nrm(ks[11], (N_LAYERS_B, d), 1.0),
        "s5_w_glu": nrm(ks[12], (N_LAYERS_B, d, 2 * d), d ** -0.5),
        "attn_w_qkv": nrm(ks[13], (N_LAYERS_C, d, QKV_WIDTH), d ** -0.5),
        "attn_q_gain": 1.0 + nrm(ks[14], (N_LAYERS_C, N_ATTN_GROUPS, HEAD_DIM), 0.1),
        "attn_k_gain": 1.0 + nrm(ks[15], (N_LAYERS_C, N_ATTN_GROUPS, HEAD_DIM), 0.1),
        "attn_w_o": nrm(ks[16], (N_LAYERS_C, ATTN_OUT, d), ATTN_OUT ** -0.5),
        "rel_bias": nrm(ks[17], (NUM_BUCKETS, N_ATTN_HEADS), 0.5),
        "ffn_w_gate_up": nrm(ks[18], (DEPTH, d, 2 * D_FF), d ** -0.5),
        "ffn_w_down": nrm(ks[19], (DEPTH, D_FF, d), D_FF ** -0.5),
    }


def reference(x, norm_mix_g, norm_ffn_g, fnet_w_out, s5_lambda_re, s5_lambda_im, s5_log_dt,
              s5_b_re, s5_b_im, s5_c_re, s5_c_im, s5_d, s5_w_glu, attn_w_qkv, attn_q_gain,
              attn_k_gain, attn_w_o, rel_bias, ffn_w_gate_up, ffn_w_down):
    counts = [0, 0, 0]
    for i in range(DEPTH):
        kind = i % N_MIXERS
        j = counts[kind]
        counts[kind] += 1
        h = rms_norm(x, norm_mix_g[i])
        if kind == 0:
            mix = fourier_mixer(h, fnet_w_out[j])
        elif kind == 1:
            mix = s5_mixer(h, s5_lambda_re[j], s5_lambda_im[j], s5_log_dt[j], s5_b_re[j], s5_b_im[j],
                           s5_c_re[j], s5_c_im[j], s5_d[j], s5_w_glu[j])
        else:
            mix = dilated_attention_mixer(h, attn_w_qkv[j], attn_q_gain[j], attn_k_gain[j],
                                          attn_w_o[j], rel_bias)
        x = x + mix.astype(x.dtype)
        h = rms_norm(x, norm_ffn_g[i])
        x = x + swiglu(h, ffn_w_gate_up[i], ffn_w_down[i]).astype(x.dtype)
    return x
```

```python
import math
import contextlib
import numpy as np
import ml_dtypes
import concourse.bass as bass
import concourse.mybir as mybir
from concourse.bass_utils import run_bass_kernel_spmd

F32 = mybir.dt.float32
BF16 = mybir.dt.bfloat16
AF = mybir.ActivationFunctionType
ALU = mybir.AluOpType

ENGS = ["pe", "act", "dve", "pool", "sp"]
NDMA = 8
SAME_ENG_WINDOW = 10 ** 9


class Op:
    __slots__ = ("eng", "fn", "deps", "dma", "idx", "epos", "sig", "tick", "dsem", "dval", "dn")


class Prog:
    def __init__(self):
        self.ops = []
        self.lastw = {}
        self.readers = {}
        self.eng_ops = {e: [] for e in ENGS}
        self.ndma = {e: 0 for e in ENGS}
        self.barrier_deps = {e: [] for e in ENGS}

    def op(self, eng, fn, reads=(), writes=(), dma=False):
        o = Op()
        o.eng, o.fn, o.dma = eng, fn, dma
        o.idx = len(self.ops)
        o.sig = False
        o.tick = 0
        deps = set()
        for k in reads:
            w = self.lastw.get(k)
            if w is not None:
                deps.add(w)
        for k in writes:
            w = self.lastw.get(k)
            if w is not None:
                deps.add(w)
            r = self.readers.get(k)
            if r:
                deps.update(r)
        for k in reads:
            self.readers.setdefault(k, []).append(o.idx)
        for k in writes:
            self.lastw[k] = o.idx
            self.readers[k] = []
        deps.discard(o.idx)
        o.epos = len(self.eng_ops[eng])
        if dma:
            o.dn = self.ndma[eng]
            self.ndma[eng] += 1
            o.dsem = o.dn % NDMA
            o.dval = 16 * (o.dn // NDMA + 1)
        need = []
        for d in deps:
            p = self.ops[d]
            if p.dma:
                need.append(d)
            elif p.eng == eng:
                if eng == "pe" and not dma:
                    continue
                if dma or (o.epos - p.epos) <= SAME_ENG_WINDOW:
                    need.append(d)
            else:
                need.append(d)
        if self.barrier_deps[eng]:
            need.extend(self.barrier_deps[eng])
            self.barrier_deps[eng] = []
        for d in need:
            p = self.ops[d]
            if not p.dma:
                p.sig = True
        o.deps = need
        self.ops.append(o)
        self.eng_ops[eng].append(o)
        return o

    def barrier(self):
        deps = []
        for e in ENGS:
            lst = self.eng_ops[e]
            comp = [o for o in lst if not o.dma]
            if comp:
                deps.append(comp[-1].idx)
            dm = [o for o in lst if o.dma]
            for o in dm[-NDMA:]:
                deps.append(o.idx)
        for e in ENGS:
            self.barrier_deps[e] = list(deps)
        self.lastw = {}
        self.readers = {}

    def emit(self, nc):
        for e in ENGS:
            t = 0
            for o in self.eng_ops[e]:
                if o.sig:
                    t += 1
                    o.tick = t
        with contextlib.ExitStack() as st:
            csem = {e: st.enter_context(nc.semaphore("c_" + e)) for e in ENGS}
            dsem = {e: [st.enter_context(nc.semaphore("d_%s%d" % (e, i))) for i in range(NDMA)]
                    for e in ENGS if self.ndma[e] > 0}
            block = st.enter_context(nc.Block())
            prog = self

            def body(ename):
                def f(eng):
                    seen = {}
                    for o in prog.eng_ops[ename]:
                        waits = {}
                        for d in o.deps:
                            p = prog.ops[d]
                            if p.dma:
                                key = ("d", p.eng, p.dsem)
                                val = p.dval
                            else:
                                key = ("c", p.eng)
                                val = p.tick
                            if waits.get(key, 0) < val:
                                waits[key] = val
                        if o.dma and o.dn >= NDMA:
                            key = ("d", ename, o.dsem)
                            val = o.dval - 16
                            if waits.get(key, 0) < val:
                                waits[key] = val
                        for key, val in waits.items():
                            if seen.get(key, 0) >= val:
                                continue
                            seen[key] = val
                            s = csem[key[1]] if key[0] == "c" else dsem[key[1]][key[2]]
                            eng.wait_ge(s, val)
                        ins = o.fn(eng)
                        if o.dma:
                            ins.then_inc(dsem[ename][o.dsem], 16)
                        elif o.sig:
                            ins.then_inc(csem[ename], 1)
                    n = prog.ndma[ename]
                    for i in range(min(n, NDMA)):
                        cnt = (n - 1 - i) // NDMA + 1
                        if seen.get(("d", ename, i), 0) < 16 * cnt:
                            eng.wait_ge(dsem[ename][i], 16 * cnt)
                return f

            block.tensor(body("pe"))
            block.scalar(body("act"))
            block.vector(body("dve"))
            block.gpsimd(body("pool"))
            block.sync(body("sp"))


S = 2048
D = 1024
DFF = 2816
KC = 8
NTB = 4
TB = 512
DEPTH = 4
EPS = 1e-6
FGROUPS = [(0, 6), (6, 6), (12, 5), (17, 5)]
GMAX = 6
ARENA_BYTES = 139 * 1024 + 1024

PARAM_SPECS = [
    ("norm_mix_g", [4, 1024]), ("norm_ffn_g", [4, 1024]), ("fnet_w_out", [2, 1024, 1024]),
    ("s5_lambda_re", [1, 2, 64, 64]), ("s5_lambda_im", [1, 2, 64, 64]), ("s5_log_dt", [1, 2, 64]),
    ("s5_b_re", [1, 2, 64, 64, 16]), ("s5_b_im", [1, 2, 64, 64, 16]),
    ("s5_c_re", [1, 2, 64, 16, 64]), ("s5_c_im", [1, 2, 64, 16, 64]),
    ("s5_d", [1, 1024]), ("s5_w_glu", [1, 1024, 2048]), ("attn_w_qkv", [1, 1024, 9216]),
    ("attn_q_gain", [1, 3, 64]), ("attn_k_gain", [1, 3, 64]), ("attn_w_o", [1, 1024, 1024]),
    ("rel_bias", [32, 48]), ("ffn_w_gate_up", [4, 1024, 5632]), ("ffn_w_down", [4, 2816, 1024]),
]


def host_consts():
    c = {}
    c["c_ident"] = np.eye(128, dtype=np.float32)
    k = np.arange(128)
    ang = 2 * np.pi * np.outer(k, k) / 128.0
    c["c_dftc"] = np.concatenate([np.cos(ang), np.sin(ang)], axis=1).astype(ml_dtypes.bfloat16)
    s = np.arange(S)
    prod = (np.outer(s, s) % S).astype(np.float64)
    angs = 2 * np.pi * prod / S
    scale = 1.0 / math.sqrt(S * 128.0)
    cs = (np.cos(angs) * scale)
    sn = (-np.sin(angs) * scale)
    def lay(t):
        o = np.zeros((4, 128, 16, 258), np.float64)
        for nbk in range(4):
            blk = t[:, nbk * 256:nbk * 256 + 257]
            o[nbk, :, :, 0:257] = blk.reshape(16, 128, 257).transpose(1, 0, 2)
        return o
    c["c_seqtab"] = np.ascontiguousarray(np.stack([lay(cs), lay(sn)], 0)).astype(ml_dtypes.bfloat16)
    def t5_bucket(rel):
        half = 16
        max_exact = 8
        n = np.abs(rel)
        sign = (rel > 0).astype(np.int32) * half
        large = max_exact + (np.log(np.maximum(n, 1) / max_exact) / math.log(1024 / max_exact)
                             * (half - max_exact)).astype(np.int32)
        large = np.minimum(large, half - 1)
        return (sign + np.where(n < max_exact, n, large)).astype(np.int32)
    u = np.arange(384)
    off = u - 191
    band = (np.abs(off) <= 64)
    oh = np.zeros((32, 3, 384), np.float32)
    for g, dil in enumerate((1, 4, 16)):
        bk = t5_bucket(off * dil)
        for ui in range(384):
            if band[ui]:
                oh[bk[ui], g, ui] = 1.0
    c["c_oh"] = oh.astype(ml_dtypes.bfloat16)
    c["c_band"] = np.broadcast_to((8.0 * band.astype(np.float32))[None, None, :], (16, 3, 384)).copy()
    c["c_bandneg"] = np.broadcast_to(((band.astype(np.float32) - 1.0) * 30000.0)[None, None, :], (16, 3, 384)).copy()
    c["c_exch"] = np.eye(128, dtype=np.float32)[::-1].copy().astype(ml_dtypes.bfloat16)
    bd = np.zeros((128, 128), np.float32)
    bd[:64, :64] = 1.0
    bd[64:, 64:] = 1.0
    c["c_bd"] = bd.astype(ml_dtypes.bfloat16)
    ex = np.zeros((2, 25), np.float32)
    kk = np.arange(8)
    ex[0, 0:8] = 7 - kk
    ex[0, 8:16] = -1 - kk
    ex[0, 16:24] = kk + 1
    ex[1, 0:8] = kk
    ex[1, 8:16] = kk - 8
    ex[1, 16:24] = 8 - kk
    ex[:, 24] = 1.0
    c["c_s5expo"] = np.broadcast_to(ex[None], (128, 2, 25)).copy()
    jj = np.arange(128) // 16
    mf = (jj[:, None] <= jj[None, :]).astype(np.float32)
    mb = (jj[:, None] >= jj[None, :]).astype(np.float32)
    c["c_s5mask"] = np.stack([np.tile(mf, (1, 4)), np.tile(mb, (1, 4))], 1).copy()
    em = np.ones((128, 4, 2, 2, 128), np.float32)
    em[0:64, 0, 0, 0, :] = 0.0
    em[64:128, 1, 1, 1, :] = 0.0
    em[0:64, 3, :, 0, :] = 0.0
    em[64:128, 3, :, 1, :] = 0.0
    c["c_ebmask"] = em.reshape(128, 4, 512).astype(ml_dtypes.bfloat16)
    return c


CONST_SPECS = [("c_ident", [128, 128], F32), ("c_dftc", [128, 256], BF16),
               ("c_seqtab", [2, 4, 128, 16, 258], BF16), ("c_oh", [32, 3, 384], BF16),
               ("c_band", [16, 3, 384], F32), ("c_bandneg", [16, 3, 384], F32), ("c_exch", [128, 128], BF16), ("c_bd", [128, 128], BF16), ("c_ebmask", [128, 4, 512], BF16),
               ("c_s5expo", [128, 2, 25], F32), ("c_s5mask", [128, 2, 512], F32)]


class K:
    def __init__(self, nc, st):
        self.nc = nc
        self.st = st
        self.P = Prog()
        self.bank = 0
        self.ev = 0

    def sb(self, name, shape, dt):
        return self.st.enter_context(self.nc.sbuf_tensor(name, shape, dt))

    def A(self, off, shape, dt):
        n = int(np.prod(shape[1:]))
        sz = 4 if dt == F32 else 2
        assert off % 4 == 0 and off + n * sz <= ARENA_BYTES, (off, shape)
        v = self.arena[:, off // 2: off // 2 + n * sz // 2]
        if dt == F32:
            v = v.bitcast(F32)
        if len(shape) > 2:
            names = " ".join("d%d" % i for i in range(1, len(shape)))
            kw = {"d%d" % i: shape[i] for i in range(1, len(shape))}
            v = v.rearrange("p (%s) -> p %s" % (names, names), **kw)
        if shape[0] != 128:
            v = v[0:shape[0]]
        return v

    def nb(self):
        b = self.bank
        self.bank = (self.bank + 1) % 8
        return b

    def mm(self, out, lhsT, rhs, start, stop, reads, writes):
        self.P.op("pe", lambda e: e.matmul(out, lhsT, rhs, start=start, stop=stop), reads, writes)

    def tr(self, out, in_, ident, reads, writes):
        self.P.op("pe", lambda e: e.transpose(out, in_, ident), reads, writes)

    def copy(self, eng, out, in_, reads, writes):
        if eng == "act":
            self.P.op("act", lambda e: e.copy(out, in_), reads, writes)
        else:
            self.P.op(eng, lambda e: e.tensor_copy(out, in_), reads, writes)

    def evac_eng(self):
        self.ev += 1
        return "act" if self.ev % 2 else "dve"

    def dma(self, q, out, in_, reads, writes, nonc=False):
        nc = self.nc
        if nonc:
            def f(e):
                with nc.allow_non_contiguous_dma(reason="small strided load"):
                    return e.dma_start(out=out, in_=in_)
        else:
            def f(e):
                return e.dma_start(out=out, in_=in_)
        self.P.op(q, f, reads, writes, dma=True)


def ph_setup(k, dr):
    P = k.P
    k.ident = k.sb("ident", [128, 128], F32)
    k.identb = k.sb("identb", [128, 128], BF16)
    k.ones = k.sb("ones", [128, 128], BF16)
    k.epsb = k.sb("epsb", [128, 1], F32)
    k.gsb = k.sb("gsb", [128, 64], F32)
    k.rstd = k.sb("rstd", [128, 512], F32)
    k.dma("sp", k.ident[:], dr["c_ident"], [], ["ident"])
    P.op("dve", lambda e: e.memset(k.ones[:], 1.0), writes=["ones"])
    P.op("dve", lambda e: e.memset(k.epsb[:], EPS), writes=["epsb"])
    P.op("dve", lambda e: e.tensor_copy(k.identb[:], k.ident[:]), reads=["ident"], writes=["identb"])
    g8 = k.A(65536, [128, 128], F32)
    k.dma("sp", g8[0:32, :], dr["norm_mix_g"].rearrange("l (c p) -> (l c) p", p=128), [], ["g8a"])
    k.dma("sp", g8[32:64, :], dr["norm_ffn_g"].rearrange("l (c p) -> (l c) p", p=128), [], ["g8b"])
    b = k.nb()
    k.tr(k.ps[b][:, 0:64], g8[0:64, :], k.ident[0:64, 0:64], ["g8a", "g8b", "ident"], [("ps", b)])
    k.copy("dve", k.gsb[:], k.ps[b][:, 0:64], [("ps", b)], ["gsb"])


def ph_load_x(k, dr):
    xin = [k.A(i * 4096, [128, 1024], F32) for i in range(2)]
    for t in range(S // 128):
        bq = t % 2
        k.dma("sp", xin[bq], dr["x"][t * 128:(t + 1) * 128, :], [], [("xin", bq)])
        for c0 in (0, 4):
            b = k.nb()
            for c in range(c0, c0 + 4):
                k.tr(k.ps[b][:, (c - c0) * 128:(c - c0 + 1) * 128], xin[bq][:, c * 128:(c + 1) * 128], k.ident[:],
                     [("xin", bq), "ident"], [("ps", b)])
            k.copy(k.evac_eng(), k.xT[:, c0:c0 + 4, t * 128:(t + 1) * 128],
                   k.ps[b][:].rearrange("p (c n) -> p c n", c=4), [("ps", b)],
                   [("xT", c, t // 4) for c in range(c0, c0 + 4)])


def ph_store(k, dr, tiles=None, xo_off=0):
    xo = [k.A(xo_off + i * 4096, [128, 1024], F32) for i in range(2)]
    for t in (range(S // 128) if tiles is None else tiles):
        bq = t % 2
        for c0 in (0, 4):
            b = k.nb()
            for c in range(c0, c0 + 4):
                k.tr(k.ps[b][:, (c - c0) * 128:(c - c0 + 1) * 128], k.xT[:, c, t * 128:(t + 1) * 128], k.ident[:],
                     [("xT", c, t // 4), "ident"], [("ps", b)])
            k.copy(k.evac_eng(), xo[bq][:, c0 * 128:(c0 + 4) * 128], k.ps[b][:], [("ps", b)], [("xo", bq, c0)])
        k.dma("sp", dr["out"][t * 128:(t + 1) * 128, :], xo[bq], [("xo", bq, 0), ("xo", bq, 4)], [("out", t)])


def rmsnorm(k, gcol, hT, hkey, sq, view=None, only_rstd=None):
    P = k.P
    for tb in range(NTB):
        ts = slice(tb * TB, (tb + 1) * TB)
        xk = [("xT", c, tb) for c in range(KC)]
        P.op("act", lambda e, ts=ts: e.activation(sq, k.xT[:, :, ts], AF.Square), reads=xk, writes=["sq"])
        b = k.nb()
        for c in range(KC):
            k.mm(k.ps[b][:], k.ones[:], sq[:, c, :], c == 0, c == KC - 1, ["sq", "ones"], [("ps", b)])
        P.op("act", lambda e, b=b: e.activation(k.rstd[:], k.ps[b][:], AF.Ln, bias=k.epsb[:], scale=1.0 / D),
             reads=[("ps", b), "epsb"], writes=["rstd"])
        P.op("act", lambda e: e.activation(k.rstd[:], k.rstd[:], AF.Exp, scale=-0.5), reads=["rstd"], writes=["rstd"])
        if only_rstd is not None:
            only_rstd(tb, ts)
            continue
        for c in range(KC):
            if view is None:
                o_, x_, r_ = hT[:, c, ts], k.xT[:, c, ts], k.rstd[:]
            else:
                o_, x_, r_ = view(c, tb, ts)
            P.op("dve", lambda e, c=c, o_=o_, x_=x_, r_=r_: e.scalar_tensor_tensor(
                o_, x_, k.gsb[:, gcol + c:gcol + c + 1], r_, ALU.mult, ALU.mult),
                reads=[("xT", c, tb), "gsb", "rstd"], writes=[(hkey, tb)])


def ffn_bufs(k):
    wgu = [k.A(32768 + i * 24576, [128, KC, 2, GMAX * 128], BF16) for i in range(2)]
    wd = [k.A(81920 + i * 12288, [128, GMAX, 1024], BF16) for i in range(2)]
    return wgu, wd


def ffn_load(k, dr, l, gi, bq):
    wgu, wd = ffn_bufs(k)
    f0, gn = FGROUPS[gi]
    for half in range(2):
        src = dr["ffn_w_gate_up"][l][:, half * DFF + f0 * 128: half * DFF + (f0 + gn) * 128].rearrange(
            "(kk p) n -> p kk n", p=128)
        k.dma("pool", wgu[bq][:, :, half, 0:gn * 128], src, [], [("wgu", bq, half)])
    src = dr["ffn_w_down"][l][f0 * 128:(f0 + gn) * 128, :].rearrange("(f p) d -> p f d", p=128)
    k.dma("pool", wd[bq][:, 0:gn, :], src, [], [("wd", bq)])


def ph_ffn(k, dr, l, first_slot=0, preloaded=False, store=False):
    P = k.P
    hT = k.A(0, [128, KC, S], BF16)
    wgu = [k.A(32768 + i * 24576, [128, KC, 2, GMAX * 128], BF16) for i in range(2)]
    wd = [k.A(81920 + i * 12288, [128, GMAX, 1024], BF16) for i in range(2)]
    act = [k.A(106496 + i * 6144, [128, GMAX, TB], BF16) for i in range(2)]
    sg = [k.A(118784 + i * 2048, [128, TB], F32) for i in range(2)]
    sq = k.A(131072, [128, KC, TB], BF16)
    wgu_d = dr["ffn_w_gate_up"]
    wd_d = dr["ffn_w_down"]

    def load(gi):
        ffn_load(k, dr, l, gi, (gi + first_slot) % 2)

    if not preloaded:
        load(0)
    rmsnorm(k, 32 + l * 8, hT, "hT", sq)
    ai = 0
    for gi, (f0, gn) in enumerate(FGROUPS):
        bq = (gi + first_slot) % 2
        if gi + 1 < len(FGROUPS):
            load(gi + 1)
        for tb in range(NTB):
            ts = slice(tb * TB, (tb + 1) * TB)
            ab = ai % 2
            ai += 1
            for fi in range(gn):
                bg = k.nb()
                for c in range(KC):
                    k.mm(k.ps[bg][:], wgu[bq][:, c, 0, fi * 128:(fi + 1) * 128], hT[:, c, ts], c == 0, c == KC - 1,
                         [("wgu", bq, 0), ("hT", tb)], [("ps", bg)])
                bu = k.nb()
                for c in range(KC):
                    k.mm(k.ps[bu][:], wgu[bq][:, c, 1, fi * 128:(fi + 1) * 128], hT[:, c, ts], c == 0, c == KC - 1,
                         [("wgu", bq, 1), ("hT", tb)], [("ps", bu)])
                sb_ = fi % 2
                P.op("act", lambda e, bg=bg, sb_=sb_: e.activation(sg[sb_], k.ps[bg][:], AF.Silu),
                     reads=[("ps", bg)], writes=[("sg", sb_)])
                P.op("dve", lambda e, bu=bu, sb_=sb_, ab=ab, fi=fi: e.tensor_tensor(
                    act[ab][:, fi, :], sg[sb_], k.ps[bu][:], ALU.mult),
                    reads=[("ps", bu), ("sg", sb_)], writes=[("act", ab, fi)])
            for dcn in range(KC):
                bd = k.nb()
                for fi in range(gn):
                    k.mm(k.ps[bd][:], wd[bq][:, fi, dcn * 128:(dcn + 1) * 128], act[ab][:, fi, :], fi == 0, fi == gn - 1,
                         [("wd", bq), ("act", ab, fi)], [("ps", bd)])
                eng = "dve"
                P.op(eng, lambda e, bd=bd, dcn=dcn, ts=ts: e.tensor_tensor(
                    k.xT[:, dcn, ts], k.xT[:, dcn, ts], k.ps[bd][:], ALU.add),
                    reads=[("ps", bd), ("xT", dcn, tb)], writes=[("xT", dcn, tb)])
            if store and gi == len(FGROUPS) - 1 and tb >= 1:
                ph_store(k, dr, tiles=range(4 * (tb - 1), 4 * tb), xo_off=122880)
    if store:
        ph_store(k, dr, tiles=range(12, 16), xo_off=122880)
    P.barrier()


def ph_fnet(k, dr, l, j):
    P = k.P
    hT = k.A(0, [128, KC, S], BF16)
    fT = hT
    Y = k.A(32768, [128, 16, 8, 256], BF16)
    tab = [[k.A(98304 + (i * 2 + w) * 8256, [128, 16, 258], BF16) for w in range(2)] for i in range(2)]
    sq = k.A(131328, [128, KC, TB], BF16)
    tmpa = [k.A(139520 + i * 1040, [128, 260], F32) for i in range(2)]
    wout = k.A(32768, [128, KC, 1024], BF16)
    dftc = k.dftc
    seqtab = dr["c_seqtab"]

    def loadtab(nbk):
        bq = nbk % 2
        for w in range(2):
            k.dma("sp", tab[bq][w], seqtab[w, nbk], [], [("tab", bq, w)])

    loadtab(0)
    rmsnorm(k, l * 8, hT, "hT", sq)
    for st_ in range(16):
        for g0 in range(0, 8, 2):
            b = k.nb()
            for g in (g0, g0 + 1):
                k.mm(k.ps[b][:, (g - g0) * 256:(g - g0 + 1) * 256], hT[:, g, st_ * 128:(st_ + 1) * 128], dftc[:],
                     True, True, [("hT", st_ // 4), "dftc"], [("ps", b)])
            k.copy(k.evac_eng(), Y[:, st_, g0:g0 + 2, :], k.ps[b][:].rearrange("p (g n) -> p g n", g=2),
                   [("ps", b)], [("Y", st_)])
    yk = [("Y", s_) for s_ in range(16)]
    fkeys = [("fT", t_) for t_ in range(NTB)]
    ti = 0
    for nbk in range(4):
        bq = nbk % 2
        if nbk + 1 < 4:
            loadtab(nbk + 1)
        for g in range(8):
            ba = k.nb()
            bb_ = k.nb()
            for w, bnk in ((0, ba), (1, bb_)):
                for st_ in range(16):
                    k.mm(k.ps[bnk][:, 0:257], Y[:, st_, g, w * 128:(w + 1) * 128], tab[bq][w][:, st_, 0:257],
                         st_ == 0, st_ == 15, yk + [("tab", bq, w)], [("ps", bnk)])
            ta = tmpa[ti % 2]
            ti += 1
            P.op("act", lambda e, ta=ta, ba=ba: e.copy(ta[:, 0:257], k.ps[ba][:, 0:257]), reads=[("ps", ba)],
                 writes=[("tmpa", ti % 2)])
            s0 = nbk * 256
            ncol = 257 if nbk == 3 else 256
            P.op("dve", lambda e, ta=ta, bb_=bb_, g=g, s0=s0, ncol=ncol: e.tensor_tensor(
                fT[:, g, s0:s0 + ncol], ta[:, 0:ncol], k.ps[bb_][:, 0:ncol], ALU.add),
                reads=[("ps", bb_), ("tmpa", ti % 2)], writes=fkeys)
            i0 = 1 if nbk == 0 else 0
            i1 = 255 if nbk == 3 else 256
            n_ = i1 - i0 + 1
            base = fT[:, g, S - (s0 + i0):S - (s0 + i0) + 1]
            mv = bass.AP(base.tensor, base.offset, [list(base.ap[0]), [-1, n_]])
            P.op("dve", lambda e, ta=ta, bb_=bb_, mv=mv, i0=i0, i1=i1: e.tensor_tensor(
                mv, ta[:, i0:i1 + 1], k.ps[bb_][:, i0:i1 + 1], ALU.subtract),
                reads=[("ps", bb_), ("tmpa", ti % 2)], writes=fkeys)
    P.barrier()
    k.dma("pool", wout, dr["fnet_w_out"][j].rearrange("(kk p) n -> p kk n", p=128), [], ["wout"])
    ffn_load(k, dr, l, 0, 1)
    for tb in range(NTB):
        ts = slice(tb * TB, (tb + 1) * TB)
        for dcn in range(KC):
            b = k.nb()
            for c in range(KC):
                k.mm(k.ps[b][:], wout[:, c, dcn * 128:(dcn + 1) * 128], fT[:, c, ts], c == 0, c == KC - 1,
                     ["wout", ("fT", tb)], [("ps", b)])
            P.op("dve", lambda e, b=b, dcn=dcn, ts=ts: e.tensor_tensor(
                k.xT[:, dcn, ts], k.xT[:, dcn, ts], k.ps[b][:], ALU.add),
                reads=[("ps", b), ("xT", dcn, tb)], writes=[("xT", dcn, tb)])
    P.barrier()


PAD = 64
CMW = S + 2 * PAD
DILS = (1, 4, 16)


def ph_attn(k, dr, l, j):
    P = k.P
    nc = k.nc
    hT = k.A(0, [128, KC, S], BF16)
    oT = k.A(32768, [128, KC, S], BF16)
    wqkv = k.A(65536, [128, KC, 384], BF16)
    cm = [[k.A(71680 + (i * 3 + w) * 4352, [128, CMW], BF16) for w in range(3)] for i in range(2)]
    vaug = [k.A(97792 + i * 4420, [128, 17, 2, 65], BF16) for i in range(2)]
    acc = [k.A(106632 + i * 8192, [128, S], F32) for i in range(2)]
    EB = [[k.A(123016 + (hh * 3 + v) * 1024, [128, 2, 256], BF16) for v in range(3)] for hh in range(2)]
    Et = [k.A(129160 + i * 1024, [128, 512], BF16) for i in range(2)]
    ebtmp = k.A(129160, [128, 512], F32)
    Pt = [k.A(131208 + i * 1024, [128, 512], BF16) for i in range(2)]
    hank = [k.A(133256 + i * 256, [128, 128], BF16) for i in range(4)]
    gtile = k.A(134280, [128, 128], F32)
    relbh = k.A(134792, [128, 48], BF16)
    relbl = k.A(134888, [128, 48], BF16)
    sqh = k.A(135304, [128, 512], BF16)
    ebm = k.A(137672, [128, 4, 512], BF16)
    exch = k.A(136328, [128, 128], BF16)
    bd = k.A(136584, [128, 128], BF16)
    qkg = k.A(136840, [128, 8], F32)
    sq = k.A(65536, [128, KC, TB], BF16)
    scr = k.attn_scr

    rmsnorm(k, l * 8, hT, "hT", sq)
    P.barrier()
    k.dma("sp", exch, dr["c_exch"], [], ["exch"])
    k.dma("sp", bd, dr["c_bd"], [], ["bd"])
    k.dma("sp", ebm, dr["c_ebmask"], [], ["ebm"])
    for i in range(2):
        for w in range(3):
            P.op("dve", lambda e, i=i, w=w: e.memset(cm[i][w], 0.0), writes=[("cm", i, w)])
        P.op("dve", lambda e, i=i: e.memset(vaug[i], 1.0), writes=[("vaug", i)])
    gt = gtile
    for w, nm in enumerate(("attn_q_gain", "attn_k_gain")):
        for hh in range(2):
            k.dma("sp", gt[w * 3:(w + 1) * 3, hh * 64:(hh + 1) * 64], dr[nm][j], [], [("gt", w, hh)])
    b = k.nb()
    k.tr(k.ps[b][:, 0:6], gt[0:6, :], k.ident[0:6, 0:6], [("gt", w, hh) for w in range(2) for hh in range(2)] + ["ident"],
         [("ps", b)])
    k.copy("dve", qkg[:, 0:6], k.ps[b][:, 0:6], [("ps", b)], ["qkg"])
    relb = k.A(106632 + 3 * 4608, [128, 48], F32)
    ohs = k.A(106632, [128, 3, 384], BF16)
    k.dma("sp", relb[0:32, :], dr["rel_bias"], [], ["relb"])
    k.dma("sp", ohs[0:32], dr["c_oh"], [], ["ohs"])
    bandt = k.A(106632 + 4608, [128, 3, 384], F32)
    earr = k.A(106632 + 4608 + 4608, [128, 3, 384], F32)
    earb = k.A(106632 + 2304, [128, 3, 384], BF16)
    k.dma("sp", bandt[0:16], dr["c_band"], [], ["bandt"])
    P.op("dve", lambda e: e.tensor_copy(relbh[0:32, :], relb[0:32, :]), reads=["relb"], writes=["relbh"])
    P.op("dve", lambda e: e.tensor_tensor(relb[0:32, :], relb[0:32, :], relbh[0:32, :], ALU.subtract),
         reads=["relb", "relbh"], writes=["relb"])
    P.op("dve", lambda e: e.tensor_copy(relbl[0:32, :], relb[0:32, :]), reads=["relb"], writes=["relbl"])
    bandn = k.A(106632 + 13824 + 256, [128, 3, 384], F32)
    k.dma("sp", bandn[0:16], dr["c_bandneg"], [], ["bandn"])
    bb = [k.nb() for g in range(3)]
    for g in range(3):
        k.mm(k.ps[bb[g]][0:16, 0:384], relbh[0:32, 16 * g:16 * g + 16], ohs[0:32, g, :], True, False,
             ["relbh", "ohs"], [("ps", bb[g])])
        k.mm(k.ps[bb[g]][0:16, 0:384], relbl[0:32, 16 * g:16 * g + 16], ohs[0:32, g, :], False, True,
             ["relbl", "ohs"], [("ps", bb[g])])
        P.op("dve", lambda e, g=g: e.tensor_tensor(earr[0:16, g, :], k.ps[bb[g]][0:16, 0:384], bandt[0:16, g, :], ALU.mult),
             reads=[("ps", bb[g]), "bandt"], writes=["earr"])
    P.op("dve", lambda e: e.tensor_tensor(earb[0:16], earr[0:16], bandn[0:16], ALU.add), reads=["earr", "bandn"],
         writes=["earb"])
    k.dma("sp", scr.ap(), earb[0:16], ["earb"], ["scr"])
    P.barrier()

    wq = dr["attn_w_qkv"][j]
    it = 0
    import os
    STOP = float(os.environ.get("ATTN_STOP", "99"))
    for hp in range(8 if STOP > 10 else (1 if STOP > 0 else 0)):
        for g in range(3):
            d = DILS[g]
            seg = S // d
            bq = it % 2
            it += 1
            for w in range(3):
                c0 = g * 3072 + w * 1024 + hp * 128
                k.dma("pool", wqkv[:, :, w * 128:(w + 1) * 128],
                      wq[:, c0:c0 + 128].rearrange("(kk p) n -> p kk n", p=128), [], [("wqkv", w)])
            if STOP <= 0.2:
                continue
            for hh in range(2):
                row = (2 * hp + hh) * 3 + g
                for ab in range(2):
                    k.dma("sp", hank[hh * 2 + ab], bass.AP(scr, row * 384 + ab * 128, [[1, 128], [1, 128]]),
                          ["scr"], [("hank", hh * 2 + ab)])
            bt = k.nb()
            for i in range(4):
                k.mm(k.ps[bt][:, i * 128:(i + 1) * 128], hank[i], exch, True, True, [("hank", i), "exch"], [("ps", bt)])
            nvar = (3, 2, 1)[g]
            if STOP <= 0.4:
                continue
            for hh in range(2):
                src = k.ps[bt][:, hh * 256:(hh + 1) * 256].unsqueeze(1).to_broadcast([128, 2, 256])
                for v in range(nvar):
                    if v == 2:
                        P.op("dve", lambda e, hh=hh, v=v, src=src: e.tensor_copy(EB[hh][v], src),
                             reads=[("ps", bt)], writes=[("EB", hh, v)])
                        continue
                    mv = 3 if g == 2 else v
                    mk = ebm[:, mv, :].rearrange("p (a b) -> p a b", a=2)
                    ebt3 = ebtmp.rearrange("p (a b) -> p a b", a=2)
                    P.op("dve", lambda e, src=src, mk=mk, ebt3=ebt3: e.scalar_tensor_tensor(
                        ebt3, src, 30000.0, mk, ALU.add, ALU.mult), reads=[("ps", bt), "ebm", "ebtmp"], writes=["ebtmp", ("Et", 0), ("Et", 1)])
                    P.op("dve", lambda e, hh=hh, v=v, ebt3=ebt3: e.tensor_scalar(EB[hh][v], ebt3, -30000.0, None, ALU.add),
                         reads=["ebtmp"], writes=[("EB", hh, v)])
            if STOP <= 0.6:
                continue
            sqhs = [sqh, k.A(141768, [128, 512], BF16)]
            rstds = [(k.rstd[:], ["rstd"]), (ebtmp, ["ebtmp", ("Et", 0), ("Et", 1)])]
            pend = []

            def stageB(rec):
                w, tb, b, ov, iv, ni = rec
                sq_ = sqhs[ni % 2]
                rs_, rkeys = rstds[ni % 2]
                b2 = k.nb()
                k.mm(k.ps[b2][:], bd, sq_, True, True, ["bd", ("sqh", ni % 2)], [("ps", b2)])
                P.op("act", lambda e: e.activation(rs_, k.ps[b2][:], AF.Ln, bias=k.epsb[:], scale=1.0 / 64),
                     reads=[("ps", b2), "epsb"] + rkeys, writes=rkeys)
                P.op("act", lambda e: e.activation(rs_, rs_, AF.Exp, scale=-0.5), reads=rkeys, writes=rkeys)
                rv = rs_.rearrange("p (m r) -> p r m", r=d)
                gc = w * 3 + g
                P.op("dve", lambda e: e.scalar_tensor_tensor(ov, iv, qkg[:, gc:gc + 1], rv, ALU.mult, ALU.mult),
                     reads=[("ps", b), "qkg"] + rkeys, writes=[("cm", bq, w)])

            ni = 0
            for w in (2, 0, 1):
                buf = cm[bq][w]
                for tb in range(NTB):
                    ts = slice(tb * TB, (tb + 1) * TB)
                    b = k.nb()
                    for c in range(KC):
                        k.mm(k.ps[b][:], wqkv[:, c, w * 128:(w + 1) * 128], hT[:, c, ts], c == 0, c == KC - 1,
                             [("wqkv", w), ("hT", tb)], [("ps", b)])
                    cnt = TB // d
                    m0 = tb * cnt
                    ov = buf[:, PAD:PAD + S].rearrange("p (r m) -> p r m", r=d)[:, :, m0:m0 + cnt]
                    iv = k.ps[b][:].rearrange("p (m r) -> p r m", r=d)
                    if w == 2:
                        P.op("act", lambda e, ov=ov, iv=iv: e.copy(ov, iv), reads=[("ps", b)], writes=[("cm", bq, w)])
                    else:
                        sq_ = sqhs[ni % 2]
                        P.op("act", lambda e, b=b, sq_=sq_: e.activation(sq_, k.ps[b][:], AF.Square), reads=[("ps", b)],
                             writes=[("sqh", ni % 2)])
                        pend.append((w, tb, b, ov, iv, ni))
                        ni += 1
                        if len(pend) == 2:
                            stageB(pend.pop(0))
            while pend:
                stageB(pend.pop(0))
            qc, kc_, vc = cm[bq]
            if STOP <= 1:
                continue
            for t0 in (0, 8, 16):
                nt = min(8, 17 - t0)
                b = k.nb()
                pv = k.ps[b][:].bitcast(BF16)
                for t in range(t0, t0 + nt):
                    k.tr(pv[:, (t - t0) * 128:(t - t0 + 1) * 128], vc[:, t * 128:(t + 1) * 128], k.identb[:],
                         [("cm", bq, 2), "identb"], [("ps", b)])
                k.copy(k.evac_eng(), vaug[bq][:, t0:t0 + nt, :, 0:64],
                       pv[:, 0:nt * 128].rearrange("p (t h e) -> p t h e", t=nt, h=2), [("ps", b)], [("vaug", bq)])
            if STOP <= 2:
                continue
            steps = [(hh, qp) for hh in range(2) for qp in range(8)]

            def emit_scores(hh, qp):
                hs = slice(64 * hh, 64 * hh + 64)
                bs = k.nb()
                if g == 0:
                    var = 0 if qp == 0 else (1 if qp == 7 else 2)
                elif g == 1:
                    var = 0 if qp % 2 == 0 else 1
                else:
                    var = 0
                ebv = EB[hh][var].rearrange("p a b -> p (a b)")
                k.mm(k.ps[bs][:], k.identb[:], ebv, True, False, ["identb", ("EB", hh, var)], [("ps", bs)])
                for qi in range(2):
                    qb = 2 * qp + qi
                    for ab in range(2):
                        k.mm(k.ps[bs][:, (qi * 2 + ab) * 128:(qi * 2 + ab + 1) * 128],
                             kc_[hs, 128 * (qb + ab):128 * (qb + ab + 1)], qc[hs, PAD + 128 * qb:PAD + 128 * (qb + 1)],
                             False, (qi == 1 and ab == 1), [("cm", bq, 0), ("cm", bq, 1)], [("ps", bs)])
                return bs

            LOOK = 3
            sbank = {}
            for i in range(LOOK):
                sbank[i] = emit_scores(*steps[i])
            bo = None
            for i, (hh, qp) in enumerate(steps):
                pb = i % 2
                bs = sbank.pop(i)
                P.op("act", lambda e, bs=bs, pb=pb: e.activation(Pt[pb], k.ps[bs][:], AF.Exp, scale=0.125),
                     reads=[("ps", bs)], writes=[("Pt", pb)])
                if i + LOOK < len(steps):
                    sbank[i + LOOK] = emit_scores(*steps[i + LOOK])
                if qp % 2 == 0:
                    bo = k.nb()
                for qi in range(2):
                    qb = 2 * qp + qi
                    slot = (qp % 2) * 2 + qi
                    for ab in range(2):
                        k.mm(k.ps[bo][0:65, slot * 128:(slot + 1) * 128], vaug[bq][:, qb + ab, hh, :],
                             Pt[pb][:, (qi * 2 + ab) * 128:(qi * 2 + ab + 1) * 128], ab == 0, ab == 1,
                             [("vaug", bq), ("Pt", pb)], [("ps", bo)])
                if qp % 2 == 1:
                    qq = qp // 2
                    a_ = acc[hh][0:65, :]
                    pin = k.ps[bo][0:65, :]
                    if d == 1:
                        av = a_[:, 512 * qq:512 * (qq + 1)]
                    elif d == 4:
                        av = a_.rearrange("p (m r) -> p m r", r=4)[:, :, qq]
                    else:
                        av = a_.rearrange("p (m r) -> p r m", r=16)[:, 4 * qq:4 * qq + 4, :]
                        pin = pin.rearrange("p (r m) -> p r m", r=4)
                    if g == 0:
                        P.op("act", lambda e, av=av, pin=pin: e.copy(av, pin), reads=[("ps", bo)], writes=[("acc", hh)])
                    else:
                        P.op("dve", lambda e, av=av, pin=pin: e.tensor_tensor(av, av, pin, ALU.add),
                             reads=[("ps", bo), ("acc", hh)], writes=[("acc", hh)])
        if hp == 7:
            wo_pre = k.A(65536, [128, KC, 1024], BF16)
            k.dma("pool", wo_pre, dr["attn_w_o"][j].rearrange("(kk p) n -> p kk n", p=128), [],
                  ["wo"] + [("wqkv", w_) for w_ in range(3)] + [("cm", 0, w_) for w_ in range(3)])
        for hh in range(2 if STOP > 3 else 0):
            P.op("act", lambda e, hh=hh: e.activation(acc[hh][64:65, :], acc[hh][64:65, :], AF.Ln),
                 reads=[("acc", hh)], writes=[("acc", hh)])
            P.op("act", lambda e, hh=hh: e.activation(acc[hh][64:65, :], acc[hh][64:65, :], AF.Exp, scale=-1.0),
                 reads=[("acc", hh)], writes=[("acc", hh)])
            for tb in range(NTB):
                ts = slice(tb * TB, (tb + 1) * TB)
                rh = Et[tb % 2]
                rl = Pt[tb % 2]
                P.op("dve", lambda e, rh=rh, hh=hh, ts=ts: e.tensor_copy(rh[64:65, :], acc[hh][64:65, ts]),
                     reads=[("acc", hh)], writes=[("Et", tb % 2), "ebtmp"])
                P.op("dve", lambda e, rh=rh, hh=hh, ts=ts: e.tensor_tensor(
                    k.rstd[64:65, :], acc[hh][64:65, ts], rh[64:65, :], ALU.subtract),
                    reads=[("acc", hh), ("Et", tb % 2)], writes=["rstd"])
                P.op("dve", lambda e, rl=rl: e.tensor_copy(rl[64:65, :], k.rstd[64:65, :]),
                     reads=["rstd"], writes=[("Pt", tb % 2)])
                b = k.nb()
                k.mm(k.ps[b][0:64, :], k.ones[64:65, 0:64], rh[64:65, :], True, False,
                     ["ones", ("Et", tb % 2)], [("ps", b)])
                k.mm(k.ps[b][0:64, :], k.ones[64:65, 0:64], rl[64:65, :], False, True,
                     ["ones", ("Pt", tb % 2)], [("ps", b)])
                P.op("dve", lambda e, hh=hh, hp=hp, ts=ts, b=b: e.tensor_tensor(
                    oT[64 * hh:64 * hh + 64, hp, ts], acc[hh][0:64, ts], k.ps[b][0:64, :], ALU.mult),
                    reads=[("ps", b), ("acc", hh)], writes=[("oT", tb)])
    P.barrier()
    wo = k.A(65536, [128, KC, 1024], BF16)
    for tb in range(NTB):
        ts = slice(tb * TB, (tb + 1) * TB)
        for dcn in range(KC):
            b = k.nb()
            for c in range(KC):
                k.mm(k.ps[b][:], wo[:, c, dcn * 128:(dcn + 1) * 128], oT[:, c, ts], c == 0, c == KC - 1,
                     ["wo", ("oT", tb)], [("ps", b)])
            P.op("dve", lambda e, b=b, dcn=dcn, ts=ts: e.tensor_tensor(
                k.xT[:, dcn, ts], k.xT[:, dcn, ts], k.ps[b][:], ALU.add),
                reads=[("ps", b), ("xT", dcn, tb)], writes=[("xT", dcn, tb)])
    P.barrier()


TWO_PI = 2.0 * math.pi
MAGIC = 12582912.0


def ph_s5(k, dr, l, j):
    P = k.P
    nc = k.nc
    Dh, D2, Vd, Td = k.s5_dh, k.s5_d2, k.s5_vd, k.s5_td
    cnt = [0]

    def uid(s_):
        cnt[0] += 1
        return "%s%d" % (s_, cnt[0])

    def tt(eng, out, a, b, op, reads, writes):
        P.op(eng, lambda e: e.tensor_tensor(out, a, b, op), reads, writes)

    hTp = k.A(0, [128, KC, 8, 256], BF16)
    sq = k.A(32768, [128, KC, TB], BF16)

    def view(c, tb, ts):
        return (hTp[:, c, :, 64 * tb:64 * tb + 64],
                k.xT[:, c, ts].rearrange("p (c j) -> p j c", j=8),
                k.rstd[:].rearrange("p (c j) -> p j c", j=8))

    rmsnorm(k, l * 8, None, "hTp", sq, view=view)
    hk = [("hTp", tb) for tb in range(NTB)]
    for kc in range(KC):
        for g8 in range(8):
            g = kc * 8 + g8
            k.dma("sp" if g8 % 2 == 0 else "act", Dh.ap()[g].rearrange("(j p) c -> p j c", p=16),
                  hTp[g8 * 16:(g8 + 1) * 16, kc, :, :], hk, [("Dh", g)])
    P.barrier()

    import os
    S5STOP = float(os.environ.get("S5_STOP", "99"))
    if S5STOP <= 0:
        return
    BIG = [128, 2, 32, 25]
    def big(i):
        return k.A(i * 6400, BIG, F32)
    AR, AI, KK, XR, SN, CS = [big(i) for i in range(6)]
    CRE, CIM = big(6), big(7)
    o0 = 8 * 6400
    lamt = k.A(o0, [128, 2, 2, 64], F32)
    LAM = k.A(o0 + 1024, [128, 2, 2, 32], F32)
    LDT = k.A(o0 + 1536, [128, 2, 32], F32)
    RDT = k.A(o0 + 1792, [128, 2, 32], F32)
    IDT = k.A(o0 + 2048, [128, 2, 32], F32)
    EXPO = k.A(o0 + 2304, [128, 2, 25], F32)
    FRE = k.A(o0 + 2504, [128, 2, 32], F32)
    FIM = k.A(o0 + 2760, [128, 2, 32], F32)
    DEN = k.A(o0 + 3016, [128, 2, 32], F32)
    TA = k.A(o0 + 3272, [128, 2, 32], F32)
    TB_ = k.A(o0 + 3528, [128, 2, 32], F32)
    MSK = k.A(o0 + 3784, [128, 2, 512], F32)
    o1 = o0 + 3784 + 4096
    ACO = [[k.A(139264 + (w * 2 + d) * 256, [128, 2, 32], F32) for w in range(2)] for d in range(2)]
    A1_all = k.A(139264, [128, 2, 2, 32], F32)
    A2_all = k.A(139776, [128, 2, 2, 32], F32)

    k.dma("sp", EXPO, dr["c_s5expo"], [], ["EXPO"])
    k.dma("sp", MSK, dr["c_s5mask"], [], ["MSK"])
    for w, nm in enumerate(("s5_lambda_re", "s5_lambda_im")):
        k.dma("sp", lamt[0:64, w], dr[nm][j].rearrange("d g n -> g d n"), [], [("lamt", w)])
    for gh in range(2):
        k.dma("sp", LDT[gh * 64:(gh + 1) * 64], dr["s5_log_dt"][j][:, gh * 32:(gh + 1) * 32].partition_broadcast(64),
              [], [("LDT", gh)])
    bl = [k.nb(), k.nb()]
    for gh in range(2):
        for w in range(2):
            for d in range(2):
                col = (w * 2 + d) * 32
                k.tr(k.ps[bl[gh]][0:64, col:col + 32], lamt[gh * 32:(gh + 1) * 32, w, d, :],
                     k.ident[gh * 32:(gh + 1) * 32, gh * 32:(gh + 1) * 32], [("lamt", w), "ident"], [("ps", bl[gh])])
        k.copy("dve", LAM[gh * 64:(gh + 1) * 64].rearrange("p w d g -> p (w d g)"),
               k.ps[bl[gh]][0:64, 0:128], [("ps", bl[gh])], [("LAM", gh)])
    P.op("act", lambda e: e.activation(LDT, LDT, AF.Exp), reads=[("LDT", 0), ("LDT", 1)], writes=["DT"])
    LK = [("LAM", 0), ("LAM", 1)]
    tt("dve", RDT, LAM[:, 0], LDT, ALU.mult, LK + ["DT"], ["RDT"])
    tt("dve", IDT, LAM[:, 1], LDT, ALU.mult, LK + ["DT"], ["IDT"])
    ex_b = EXPO.unsqueeze(2).to_broadcast(BIG)
    tt("dve", AR, RDT.unsqueeze(3).to_broadcast(BIG), ex_b, ALU.mult, ["RDT", "EXPO"], ["AR"])
    tt("dve", AI, IDT.unsqueeze(3).to_broadcast(BIG), ex_b, ALU.mult, ["IDT", "EXPO"], ["AI"])
    P.op("act", lambda e: e.activation(AR, AR, AF.Exp), reads=["AR"], writes=["AR"])

    if S5STOP <= 0.1:
        P.barrier()
        return

    def sinred(out, shift, okey):
        P.op("dve", lambda e: e.tensor_scalar(KK, AI, shift, 1.0 / TWO_PI, ALU.add, ALU.mult),
             reads=["AI"], writes=["KK"])
        P.op("dve", lambda e: e.tensor_scalar(KK, KK, MAGIC, None, ALU.add), reads=["KK"], writes=["KK"])
        P.op("dve", lambda e: e.tensor_scalar(KK, KK, MAGIC, None, ALU.subtract), reads=["KK"], writes=["KK"])
        P.op("dve", lambda e: e.scalar_tensor_tensor(XR, KK, -TWO_PI, AI, ALU.mult, ALU.add), reads=["KK", "AI"],
             writes=["XR"])
        P.op("dve", lambda e: e.tensor_scalar(XR, XR, shift, 3.1415925, ALU.add, ALU.min), reads=["XR"], writes=["XR"])
        P.op("dve", lambda e: e.tensor_scalar(XR, XR, -3.1415925, None, ALU.max), reads=["XR"], writes=["XR"])
        P.op("act", lambda e: e.activation(out, XR, AF.Sin), reads=["XR"], writes=[okey])

    sinred(SN, 0.0, "SN")
    sinred(CS, math.pi / 2.0, "CS")
    if S5STOP <= 0.2:
        P.barrier()
        return
    tt("dve", CRE, AR, CS, ALU.mult, ["AR", "CS"], ["CRE"])
    tt("dve", CIM, AR, SN, ALU.mult, ["AR", "SN"], ["CIM"])
    LR, LI = LAM[:, 0], LAM[:, 1]
    P1R, P1I = CRE[:, :, :, 24], CIM[:, :, :, 24]
    P.op("dve", lambda e: e.tensor_scalar(TA, P1R, -1.0, None, ALU.add), reads=["CRE"], writes=["TA"])
    tt("dve", DEN, LR, LR, ALU.mult, LK, ["DEN"])
    tt("dve", TB_, LI, LI, ALU.mult, LK, ["TB"])
    tt("dve", DEN, DEN, TB_, ALU.add, ["DEN", "TB"], ["DEN"])
    P.op("dve", lambda e: e.reciprocal(DEN, DEN), reads=["DEN"], writes=["DEN"])
    tt("dve", FRE, TA, LR, ALU.mult, ["TA"] + LK, ["FRE"])
    tt("dve", TB_, P1I, LI, ALU.mult, ["CIM"] + LK, ["TB"])
    tt("dve", FRE, FRE, TB_, ALU.add, ["FRE", "TB"], ["FRE"])
    tt("dve", FRE, FRE, DEN, ALU.mult, ["FRE", "DEN"], ["FRE"])
    tt("dve", FIM, P1I, LR, ALU.mult, ["CIM"] + LK, ["FIM"])
    tt("dve", TB_, TA, LI, ALU.mult, ["TA"] + LK, ["TB"])
    tt("dve", FIM, FIM, TB_, ALU.subtract, ["FIM", "TB"], ["FIM"])
    tt("dve", FIM, FIM, DEN, ALU.mult, ["FIM", "DEN"], ["FIM"])
    for d in range(2):
        idx = 23 if d == 0 else 16
        a1, a2 = ACO[d]
        for h_ in range(2):
            P.op("dve", lambda e, a1=a1, h_=h_, d=d, idx=idx: e.tensor_copy(a1[:, h_, :], CRE[:, d, :, idx]),
                 reads=["CRE"], writes=[("ACO", d, 0)])
        P.op("dve", lambda e, a2=a2, d=d, idx=idx: e.tensor_scalar(a2[:, 0, :], CIM[:, d, :, idx], -1.0, None, ALU.mult),
             reads=["CIM"], writes=[("ACO", d, 1)])
        P.op("dve", lambda e, a2=a2, d=d, idx=idx: e.tensor_copy(a2[:, 1, :], CIM[:, d, :, idx]),
             reads=["CIM"], writes=[("ACO", d, 1)])
    S16 = [128, 2, 32, 16]
    fr_b = FRE.unsqueeze(3).to_broadcast(S16)
    fi_b = FIM.unsqueeze(3).to_broadcast(S16)
    q_r, q_i = CRE[:, :, :, 0:16], CIM[:, :, :, 0:16]
    t_a, t_b = AR[:, :, :, 0:16], AI[:, :, :, 0:16]
    tt("dve", t_a, q_r, fi_b, ALU.mult, ["CRE", "FIM", "AR"], ["AR"])
    tt("dve", t_b, q_i, fi_b, ALU.mult, ["CIM", "FIM", "AI"], ["AI"])
    tt("dve", q_r, q_r, fr_b, ALU.mult, ["CRE", "FRE"], ["CRE"])
    tt("dve", q_i, q_i, fr_b, ALU.mult, ["CIM", "FRE"], ["CIM"])
    tt("dve", q_r, q_r, t_b, ALU.subtract, ["CRE", "AI"], ["CRE"])
    tt("dve", q_i, q_i, t_a, ALU.add, ["CIM", "AR"], ["CIM"])
    if S5STOP <= 0.3:
        P.barrier()
        return
    P.barrier()
    B_sb = k.A(0, [128, 2, 2, 32, 16], F32)
    C_sb = k.A(8192, [128, 2, 2, 32, 16], F32)
    C_nat = k.A(16384, [128, 2, 2, 8, 64], F32)
    for d in range(2):
        for ri, nm in enumerate(("s5_b_re", "s5_b_im")):
            for gh in range(2):
                k.dma("sp", B_sb[gh * 64:(gh + 1) * 64, d, ri], dr[nm][j][d][gh * 32:(gh + 1) * 32].rearrange("g n p -> n g p"),
                      [], [("B_sb", d, ri, gh)])
        for ri, nm in enumerate(("s5_c_re", "s5_c_im")):
            k.dma("sp", C_nat[:, d, ri], dr[nm][j][d].rearrange("(gb g8) po n -> (g8 po) gb n", g8=8), [], [("C_nat", d, ri)])
    for d in range(2):
        for ri in range(2):
            for gh in range(2):
                b = k.nb()
                for q4 in range(4):
                    gb = gh * 4 + q4
                    k.tr(k.ps[b][0:64, q4 * 128:(q4 + 1) * 128], C_nat[:, d, ri, gb, :], k.ident[:],
                         [("C_nat", d, ri), "ident"], [("ps", b)])
                k.copy(k.evac_eng(), C_sb[gh * 64:(gh + 1) * 64, d, ri].rearrange("p g q -> p (g q)"),
                       k.ps[b][0:64, :], [("ps", b)], [("C_sb", d, ri, gh)])
    bkeys = [("B_sb", d, ri, gh) for d in range(2) for ri in range(2) for gh in range(2)]
    ckeys = [("C_sb", d, ri, gh) for d in range(2) for ri in range(2) for gh in range(2)]
    if S5STOP <= 0.6:
        P.barrier()
        return
    W1 = [k.A(65536 + d * 16384, [128, 64, 128], BF16) for d in range(2)]
    Q4 = [128, 8, 8, 16]
    t1 = k.A(24576, Q4, F32)
    t2 = k.A(28672, Q4, F32)
    TQ = k.A(32768, [128, 2, 8, 128], BF16)
    WQ1 = k.A(98304, [128, 2, 2, 8, 128], BF16)
    WQ2 = k.A(106496, [128, 2, 2, 8, 128], BF16)
    VQ = k.A(114688, [128, 2, 2, 8, 128], BF16)

    def cplx(out_re, out_im, d, slot, X, q, neg_im, xkeys, okey):
        qs = slice(q * 8, q * 8 + 8)
        cr = CRE[:, d, qs, slot * 8:slot * 8 + 8].unsqueeze(3).to_broadcast(Q4)
        ci = CIM[:, d, qs, slot * 8:slot * 8 + 8].unsqueeze(3).to_broadcast(Q4)
        xr = X[:, d, 0, qs, :].unsqueeze(2).to_broadcast(Q4)
        xi = X[:, d, 1, qs, :].unsqueeze(2).to_broadcast(Q4)
        o_r = out_re.rearrange("p g (a b) -> p g a b", a=8)
        o_i = out_im.rearrange("p g (a b) -> p g a b", a=8)
        rd = ["CRE", "CIM"] + xkeys
        tt("dve", t1, cr, xr, ALU.mult, rd + ["t1"], ["t1"])
        tt("dve", t2, ci, xi, ALU.mult, rd + ["t2"], ["t2"])
        tt("dve", o_r, t1, t2, ALU.subtract, ["t1", "t2"], [okey])
        tt("dve", t1, cr, xi, ALU.mult, rd + ["t1"], ["t1"])
        tt("dve", t2, ci, xr, ALU.mult, rd + ["t2"], ["t2"])
        if neg_im:
            P.op("dve", lambda e: e.scalar_tensor_tensor(o_i, t1, -1.0, t2, ALU.mult, ALU.subtract),
                 reads=["t1", "t2"], writes=[okey])
        else:
            tt("dve", o_i, t1, t2, ALU.add, ["t1", "t2"], [okey])

    for q in range(4):
        for d in range(2):
            cplx(WQ1[:, d, 0], WQ1[:, d, 1], d, 0, B_sb, q, False, bkeys, ("WQ1", d))
            cplx(WQ2[:, d, 0], WQ2[:, d, 1], d, 1, B_sb, q, False, bkeys, ("WQ2", d))
            cplx(VQ[:, d, 0], VQ[:, d, 1], d, 2, C_sb, q, True, ckeys, ("VQ", d))
        for d in range(2):
            for gh in range(2):
                b = k.nb()
                pv = k.ps[b][:].bitcast(BF16)
                for g8 in range(8):
                    for ri in range(2):
                        col = (g8 * 2 + ri) * 64
                        k.tr(pv[:, col:col + 64], WQ1[gh * 64:(gh + 1) * 64, d, ri, g8, :],
                             k.identb[gh * 64:(gh + 1) * 64, gh * 64:(gh + 1) * 64], [("WQ1", d), "identb"], [("ps", b)])
                g0 = gh * 32 + q * 8
                k.copy(k.evac_eng(), W1[d][:, g0:g0 + 8, :], pv.rearrange("p (g c) -> p g c", g=8), [("ps", b)],
                       [("W1", d)])
        for d in range(2):
            for gh in range(2):
                g0 = gh * 32 + q * 8
                for ri in range(2):
                    k.dma("sp", Vd.ap()[d, g0:g0 + 8].rearrange("g (ri n) q -> n ri g q", ri=2)[:, ri],
                          VQ[gh * 64:(gh + 1) * 64, d, ri], [("VQ", d)], [("Vd", d, gh, q, ri)])
        for gh in range(2):
            hs = slice(gh * 64, (gh + 1) * 64)
            for g4 in range(2):
                bf_ = k.nb()
                bb_ = k.nb()
                for d, bnk in ((0, bf_), (1, bb_)):
                    for gi in range(4):
                        g8 = g4 * 4 + gi
                        for ri in range(2):
                            k.mm(k.ps[bnk][:, gi * 128:(gi + 1) * 128], WQ2[hs, d, ri, g8, :], VQ[hs, d, ri, g8, :],
                                 ri == 0, ri == 1, [("WQ2", d), ("VQ", d)], [("ps", bnk)])
                tf = k.A(122880, [128, 512], F32)
                tb_ = k.A(124928, [128, 512], F32)
                tt("dve", tf, k.ps[bf_][:], MSK[:, 0, :], ALU.mult, [("ps", bf_), "MSK", "tf"], ["tf"])
                tt("dve", tb_, k.ps[bb_][:], MSK[:, 1, :], ALU.mult, [("ps", bb_), "MSK", "tb_"], ["tb_"])
                tt("dve", TQ[:, gh, g4 * 4:g4 * 4 + 4, :].rearrange("p g c -> p (g c)"), tf, tb_, ALU.add,
                   ["tf", "tb_"], [("TQ", gh)])
            g0 = gh * 32 + q * 8
            k.dma("sp", Td.ap()[g0:g0 + 8].rearrange("g a b -> a g b"), TQ[:, gh], [("TQ", gh)], [("Td", gh, q)])
    P.barrier()

    if S5STOP <= 1:
        return
    S_all = k.A(0, [128, 2, 2, 32, 256], BF16)
    Ust = [[k.A(98304 + (i * 2 + gh) * 4096, [128, 8, 256], BF16) for gh in range(2)] for i in range(2)]
    for pb in range(4):
        bq = pb % 2
        for gh in range(2):
            gb = pb + 4 * gh
            k.dma("sp", Ust[bq][gh], Dh.ap()[gb * 8:(gb + 1) * 8].rearrange("g a c -> a g c"), [], [("Ust", bq, gh)])
        for g8 in range(8):
            g32 = pb * 8 + g8
            for d in range(2):
                b = k.nb()
                for gh in range(2):
                    g = gh * 32 + g32
                    for ri in range(2):
                        k.mm(k.ps[b][gh * 64:(gh + 1) * 64, ri * 256:(ri + 1) * 256], W1[d][:, g, ri * 64:(ri + 1) * 64],
                             Ust[bq][gh][:, g8, :], True, True, [("Ust", bq, gh), ("W1", d)], [("ps", b)])
                iv = k.ps[b][:].rearrange("p (ri c) -> p ri c", ri=2)
                if d == 0:
                    ov = S_all[:, 0, :, g32, :]
                else:
                    base = S_all[:, 1, 0, g32, 255:256]
                    ov = bass.AP(base.tensor, base.offset, [list(base.ap[0]), [32 * 256, 2], [-1, 256]])
                k.copy(k.evac_eng(), ov, iv, [("ps", b)], [("S", g32 % 4)])
    P.barrier()

    if S5STOP <= 2:
        return
    Hist = [k.A(65536 + d * 32896, [128, 2, 32, 257], BF16) for d in range(2)]
    Z = [k.A(131328 + i * 512, [128, 2, 2, 32], F32) for i in range(2)]
    T1 = k.A(131328 + 1024, [128, 2, 2, 32], F32)
    T2 = k.A(131328 + 1536, [128, 2, 2, 32], F32)
    for i in range(2):
        P.op("dve", lambda e, i=i: e.memset(Z[i], 0.0), writes=[("Z", i)])
    for d in range(2):
        colz = 0 if d == 0 else 256
        P.op("dve", lambda e, d=d, colz=colz: e.memset(Hist[d][:, :, :, colz], 0.0), writes=[("Hist", d, "z")])
    skeys = [("S", i) for i in range(4)]

    def swapped(z):
        base = z[:, :, 1:2, :]
        return bass.AP(base.tensor, base.offset, [list(base.ap[0]), list(base.ap[1]), [-32, 2], [1, 32]])

    for s_ in range(256):
        zi, zo = s_ % 2, (s_ + 1) % 2
        tt("dve", T1, A1_all, Z[zi], ALU.mult, [("Z", zi), ("ACO", 0, 0), ("ACO", 1, 0), "T1"], ["T1"])
        tt("dve", T2, A2_all, swapped(Z[zi]), ALU.mult, [("Z", zi), ("ACO", 0, 1), ("ACO", 1, 1), "T2"], ["T2"])
        tt("dve", T1, T1, T2, ALU.add, ["T1", "T2"], ["T1"])
        tt("dve", Z[zo], T1, S_all[:, :, :, :, s_], ALU.add, ["T1", ("Z", zo)] + skeys, [("Z", zo)])
        for d in range(2):
            col = s_ + 1 if d == 0 else 255 - s_
            P.op("act", lambda e, d=d, zo=zo, col=col: e.copy(Hist[d][:, :, :, col], Z[zo][:, d, :, :]),
                 reads=[("Z", zo)], writes=[("Hist", d, s_ % 8)])
    P.barrier()

    if S5STOP <= 3:
        return
    Ust = [k.A(0 + i * 4096, [128, 8, 256], BF16) for i in range(2)]
    Tst = [k.A(8192 + i * 2048, [128, 8, 128], BF16) for i in range(2)]
    Vst = [k.A(12288 + i * 8192, [128, 2, 2, 8, 128], BF16) for i in range(2)]
    Yst = [k.A(28672 + i * 2048, [128, 4, 256], BF16) for i in range(2)]
    wglu = k.A(32768, [128, KC, 2048], BF16)
    k.dma("pool", wglu, dr["s5_w_glu"][j].rearrange("(kk p) n -> p kk n", p=128), [], ["wglu"])
    for gb in range(8):
        bq = gb % 2
        gh = gb // 4
        hs = slice(gh * 64, (gh + 1) * 64)
        k.dma("sp", Ust[bq], Dh.ap()[gb * 8:(gb + 1) * 8].rearrange("g a c -> a g c"), [], [("Ust", bq)])
        k.dma("sp", Tst[bq], Td.ap()[gb * 8:(gb + 1) * 8].rearrange("g a b -> a g b"), [], [("Tst", bq)])
        for d in range(2):
            for ri in range(2):
                k.dma("sp", Vst[bq][hs, d, ri], Vd.ap()[d, gb * 8:(gb + 1) * 8].rearrange("g (ri n) q -> n ri g q", ri=2)[:, ri],
                      [], [("Vst", bq, d, ri)])
        for g2 in range(4):
            b = k.nb()
            for gi in range(2):
                g8 = g2 * 2 + gi
                g32 = (gb * 8 + g8) % 32
                o = k.ps[b][:, gi * 256:(gi + 1) * 256]
                k.mm(o, Tst[bq][:, g8, :], Ust[bq][:, g8, :], True, False, [("Tst", bq), ("Ust", bq)], [("ps", b)])
                for d in range(2):
                    c0 = 0 if d == 0 else 1
                    for ri in range(2):
                        k.mm(o, Vst[bq][hs, d, ri, g8, :], Hist[d][hs, ri, g32, c0:c0 + 256], False,
                             (d == 1 and ri == 1), [("Vst", bq, d, ri)], [("ps", b)])
            yb = g2 // 2
            k.copy(k.evac_eng(), Yst[yb][:, (g2 % 2) * 2:(g2 % 2) * 2 + 2, :], k.ps[b][:].rearrange("p (g c) -> p g c", g=2),
                   [("ps", b)], [("Yst", yb)])
            if g2 % 2 == 1:
                g0 = gb * 8 + yb * 4
                k.dma("sp", D2.ap()[g0:g0 + 4].rearrange("g a c -> a g c"), Yst[yb], [("Yst", yb)], [("D2", gb, yb)])
    P.barrier()

    if S5STOP <= 4:
        return
    yTp = k.A(0, [128, KC, 8, 256], BF16)
    gT = k.A(65536, [128, KC, S], BF16)
    sq = k.A(98304, [128, KC, TB], BF16)
    tmpu = k.A(106496, [128, 512], F32)
    tmpw = k.A(110592, [128, 512], F32)
    gd = k.A(118784, [128, 8], F32)
    dsb = k.A(118816, [128, 8], F32)
    d8 = k.A(118848, [128, 128], F32)
    oneb = k.A(119360, [128, 1], F32)
    P.op("dve", lambda e: e.memset(oneb, 1.0), writes=["oneb"])
    k.dma("sp", d8[0:8, :], dr["s5_d"][j].rearrange("(c p) -> c p", p=128), [], ["d8"])
    b = k.nb()
    k.tr(k.ps[b][:, 0:8], d8[0:8, :], k.ident[0:8, 0:8], ["d8", "ident"], [("ps", b)])
    k.copy("dve", dsb, k.ps[b][:, 0:8], [("ps", b)], ["dsb"])
    tt("dve", gd, dsb, k.gsb[:, l * 8:l * 8 + 8], ALU.mult, ["dsb", "gsb"], ["gd"])
    for kc in range(KC):
        for g8 in range(8):
            g = kc * 8 + g8
            k.dma("sp" if g8 % 2 == 0 else "act", yTp[g8 * 16:(g8 + 1) * 16, kc, :, :],
                  D2.ap()[g].rearrange("(i p) c -> p i c", p=16), [], [("yTp", kc)])

    tmps = [[k.A(106496 + (i * 3 + w) * 2048, [128, 512], F32) for w in range(3)] for i in range(2)]

    def after_rstd(tb, ts):
        for kc in range(KC):
            i_ = kc % 2
            tu, tv, tw = tmps[i_]
            ku, kv, kw = ("tmpu", i_), ("tmpv", i_), ("tmpw", i_)
            yv = yTp[:, kc, :, 64 * tb:64 * tb + 64].rearrange("p i c -> p c i")
            v3 = lambda t_: t_.rearrange("p (c i) -> p c i", i=8)
            tt("dve", tu, k.xT[:, kc, ts], k.rstd[:], ALU.mult, [("xT", kc, tb), "rstd", ku], [ku])
            P.op("dve", lambda e, kc=kc, yv=yv, tu=tu: e.scalar_tensor_tensor(v3(tu), v3(tu), gd[:, kc:kc + 1], yv,
                                                                           ALU.mult, ALU.add),
                 reads=[ku, "gd", ("yTp", kc)], writes=[ku])
            P.op("act", lambda e, tu=tu, tv=tv: e.activation(tv, tu, AF.Square), reads=[ku, kv], writes=[kv])
            P.op("act", lambda e, tv=tv: e.activation(tv, tv, AF.Identity, bias=oneb, scale=0.044715), reads=[kv, "oneb"],
                 writes=[kv])
            tt("dve", tv, tv, tu, ALU.mult, [kv, ku], [kv])
            P.op("act", lambda e, tv=tv, tw=tw: e.activation(tw, tv, AF.Sigmoid, scale=1.5957691216057308), reads=[kv, kw],
                 writes=[kw])
            tt("dve", gT[:, kc, ts], tu, tw, ALU.mult, [ku, kw], [("gT", tb)])


    def glu_tb(tb, ts):
        for dcn in range(KC):
            bv = k.nb()
            for c in range(KC):
                k.mm(k.ps[bv][:], wglu[:, c, dcn * 128:(dcn + 1) * 128], gT[:, c, ts], c == 0, c == KC - 1,
                     ["wglu", ("gT", tb)], [("ps", bv)])
            bg = k.nb()
            for c in range(KC):
                k.mm(k.ps[bg][:], wglu[:, c, 1024 + dcn * 128:1024 + (dcn + 1) * 128], gT[:, c, ts], c == 0, c == KC - 1,
                     ["wglu", ("gT", tb)], [("ps", bg)])
            gi_ = dcn % 2
            gw = k.A(119424 + gi_ * 4096, [128, 512], F32)
            gu = k.A(119424 + gi_ * 4096 + 2048, [128, 512], F32)
            P.op("act", lambda e, bg=bg, gw=gw: e.activation(gw, k.ps[bg][:], AF.Sigmoid), reads=[("ps", bg), ("gw", gi_)],
                 writes=[("gw", gi_)])
            tt("dve", gu, k.ps[bv][:], gw, ALU.mult, [("ps", bv), ("gw", gi_), ("gu", gi_)], [("gu", gi_)])
            tt("dve", k.xT[:, dcn, ts], k.xT[:, dcn, ts], gu, ALU.add, [("gu", gi_), ("xT", dcn, tb)], [("xT", dcn, tb)])

    pending = []

    def cb(tb, ts):
        after_rstd(tb, ts)
        if pending:
            glu_tb(*pending.pop(0))
        pending.append((tb, ts))

    rmsnorm(k, l * 8, None, None, sq, only_rstd=cb)
    while pending:
        glu_tb(*pending.pop(0))
    P.barrier()


def build(plan):
    nc = bass.Bass("TRN2", target_bir_lowering=False)
    dr = {}
    dr["x"] = nc.dram_tensor("x", [S, D], F32, kind="ExternalInput").ap()
    for name, shp in PARAM_SPECS:
        dr[name] = nc.dram_tensor(name, shp, F32, kind="ExternalInput").ap()
    for name, shp, dt in CONST_SPECS:
        dr[name] = nc.dram_tensor(name, shp, dt, kind="ExternalInput").ap()
    dr["out"] = nc.dram_tensor("out", [S, D], F32, kind="ExternalOutput").ap()
    with contextlib.ExitStack() as st:
        k = K(nc, st)
        k.xT = k.sb("xT", [128, KC, S], F32)
        k.arena = k.sb("arena", [128, ARENA_BYTES // 2], BF16)
        k.dftc = k.sb("dftc", [128, 256], BF16)
        k.ps = [st.enter_context(nc.psum_tensor("ps%d" % i, [128, 512], F32)) for i in range(8)]
        k.attn_scr = nc.dram_tensor("attn_scr", [16, 3, 384], BF16, kind="Internal")
        k.s5_dh = nc.dram_tensor("s5_dh", [64, 128, 256], BF16, kind="Internal")
        k.s5_d2 = nc.dram_tensor("s5_d2", [64, 128, 256], BF16, kind="Internal")
        k.s5_vd = nc.dram_tensor("s5_vd", [2, 64, 128, 128], BF16, kind="Internal")
        k.s5_td = nc.dram_tensor("s5_td", [64, 128, 128], BF16, kind="Internal")
        k.dma("sp", k.dftc[:], dr["c_dftc"], [], ["dftc"])
        ph_setup(k, dr)
        ph_load_x(k, dr)
        k.P.barrier()
        stored = False
        for pi_, item in enumerate(plan):
            kind, l, j = item
            if kind == "fnet":
                ph_fnet(k, dr, l, j)
            elif kind == "ffn":
                after_fnet = (pi_ > 0 and plan[pi_ - 1][0] == "fnet" and plan[pi_ - 1][1] == l)
                last = (pi_ == len(plan) - 1)
                ph_ffn(k, dr, l, first_slot=1 if after_fnet else 0, preloaded=after_fnet, store=last)
                stored = stored or last
            elif kind == "attn":
                ph_attn(k, dr, l, j)
            elif kind == "s5":
                ph_s5(k, dr, l, j)
            else:
                raise ValueError(kind)
        if not stored:
            ph_store(k, dr)
        k.P.emit(nc)
    return nc


FULL_PLAN = [("fnet", 0, 0), ("ffn", 0, 0), ("s5", 1, 0), ("ffn", 1, 0),
             ("attn", 2, 0), ("ffn", 2, 0), ("fnet", 3, 1), ("ffn", 3, 0)]

_CACHE = {}


def run(inputs, plan):
    key = tuple(plan)
    if key not in _CACHE:
        _CACHE[key] = (build(plan), host_consts())
    nc, consts = _CACHE[key]
    x = np.ascontiguousarray(np.asarray(inputs["x"], dtype=np.float32))
    n = x.shape[0]
    in_maps = []
    for i in range(n):
        m = {"x": x[i]}
        for name, shp in PARAM_SPECS:
            m[name] = np.ascontiguousarray(np.asarray(inputs[name], dtype=np.float32))
        m.update(consts)
        in_maps.append(m)
    res = run_bass_kernel_spmd(nc, in_maps, core_ids=list(range(n)))
    return np.stack([r["out"] for r in res.results], axis=0)


def kernel(**inputs):
    return run(inputs, FULL_PLAN)
```

```python
import math
import contextlib
import numpy as np
import ml_dtypes
import concourse.bass as bass
import concourse.mybir as mybir
from concourse.bass_utils import run_bass_kernel_spmd

F32 = mybir.dt.float32
BF16 = mybir.dt.bfloat16
AF = mybir.ActivationFunctionType
ALU = mybir.AluOpType

ENGS = ["pe", "act", "dve", "pool", "sp"]
NDMA = 8
SAME_ENG_WINDOW = 10 ** 9


class Op:
    __slots__ = ("eng", "fn", "deps", "dma", "idx", "epos", "sig", "tick", "dsem", "dval", "dn")


class Prog:
    def __init__(self):
        self.ops = []
        self.lastw = {}
        self.readers = {}
        self.eng_ops = {e: [] for e in ENGS}
        self.ndma = {e: 0 for e in ENGS}
        self.barrier_deps = {e: [] for e in ENGS}

    def op(self, eng, fn, reads=(), writes=(), dma=False):
        o = Op()
        o.eng, o.fn, o.dma = eng, fn, dma
        o.idx = len(self.ops)
        o.sig = False
        o.tick = 0
        deps = set()
        for k in reads:
            w = self.lastw.get(k)
            if w is not None:
                deps.add(w)
        for k in writes:
            w = self.lastw.get(k)
            if w is not None:
                deps.add(w)
            r = self.readers.get(k)
            if r:
                deps.update(r)
        for k in reads:
            self.readers.setdefault(k, []).append(o.idx)
        for k in writes:
            self.lastw[k] = o.idx
            self.readers[k] = []
        deps.discard(o.idx)
        o.epos = len(self.eng_ops[eng])
        if dma:
            o.dn = self.ndma[eng]
            self.ndma[eng] += 1
            o.dsem = o.dn % NDMA
            o.dval = 16 * (o.dn // NDMA + 1)
        need = []
        for d in deps:
            p = self.ops[d]
            if p.dma:
                need.append(d)
            elif p.eng == eng:
                if eng == "pe" and not dma:
                    continue
                if dma or (o.epos - p.epos) <= SAME_ENG_WINDOW:
                    need.append(d)
            else:
                need.append(d)
        if self.barrier_deps[eng]:
            need.extend(self.barrier_deps[eng])
            self.barrier_deps[eng] = []
        for d in need:
            p = self.ops[d]
            if not p.dma:
                p.sig = True
        o.deps = need
        self.ops.append(o)
        self.eng_ops[eng].append(o)
        return o

    def barrier(self):
        deps = []
        for e in ENGS:
            lst = self.eng_ops[e]
            comp = [o for o in lst if not o.dma]
            if comp:
                deps.append(comp[-1].idx)
            dm = [o for o in lst if o.dma]
            for o in dm[-NDMA:]:
                deps.append(o.idx)
        for e in ENGS:
            self.barrier_deps[e] = list(deps)
        self.lastw = {}
        self.readers = {}

    def emit(self, nc):
        for e in ENGS:
            t = 0
            for o in self.eng_ops[e]:
                if o.sig:
                    t += 1
                    o.tick = t
        with contextlib.ExitStack() as st:
            csem = {e: st.enter_context(nc.semaphore("c_" + e)) for e in ENGS}
            dsem = {e: [st.enter_context(nc.semaphore("d_%s%d" % (e, i))) for i in range(NDMA)]
                    for e in ENGS if self.ndma[e] > 0}
            block = st.enter_context(nc.Block())
            prog = self

            def body(ename):
                def f(eng):
                    seen = {}
                    for o in prog.eng_ops[ename]:
                        waits = {}
                        for d in o.deps:
                            p = prog.ops[d]
                            if p.dma:
                                key = ("d", p.eng, p.dsem)
                                val = p.dval
                            else:
                                key = ("c", p.eng)
                                val = p.tick
                            if waits.get(key, 0) < val:
                                waits[key] = val
                        if o.dma and o.dn >= NDMA:
                            key = ("d", ename, o.dsem)
                            val = o.dval - 16
                            if waits.get(key, 0) < val:
                                waits[key] = val
                        for key, val in waits.items():
                            if seen.get(key, 0) >= val:
                                continue
                            seen[key] = val
                            s = csem[key[1]] if key[0] == "c" else dsem[key[1]][key[2]]
                            eng.wait_ge(s, val)
                        ins = o.fn(eng)
                        if o.dma:
                            ins.then_inc(dsem[ename][o.dsem], 16)
                        elif o.sig:
                            ins.then_inc(csem[ename], 1)
                    n = prog.ndma[ename]
                    for i in range(min(n, NDMA)):
                        cnt = (n - 1 - i) // NDMA + 1
                        if seen.get(("d", ename, i), 0) < 16 * cnt:
                            eng.wait_ge(dsem[ename][i], 16 * cnt)
                return f

            block.tensor(body("pe"))
            block.scalar(body("act"))
            block.vector(body("dve"))
            block.gpsimd(body("pool"))
            block.sync(body("sp"))


S = 2048
D = 1024
DFF = 2816
KC = 8
NTB = 4
TB = 512
DEPTH = 4
EPS = 1e-6
FGROUPS = [(0, 6), (6, 6), (12, 5), (17, 5)]
GMAX = 6
ARENA_BYTES = 139 * 1024 + 1024

PARAM_SPECS = [
    ("norm_mix_g", [4, 1024]), ("norm_ffn_g", [4, 1024]), ("fnet_w_out", [2, 1024, 1024]),
    ("s5_lambda_re", [1, 2, 64, 64]), ("s5_lambda_im", [1, 2, 64, 64]), ("s5_log_dt", [1, 2, 64]),
    ("s5_b_re", [1, 2, 64, 64, 16]), ("s5_b_im", [1, 2, 64, 64, 16]),
    ("s5_c_re", [1, 2, 64, 16, 64]), ("s5_c_im", [1, 2, 64, 16, 64]),
    ("s5_d", [1, 1024]), ("s5_w_glu", [1, 1024, 2048]), ("attn_w_qkv", [1, 1024, 9216]),
    ("attn_q_gain", [1, 3, 64]), ("attn_k_gain", [1, 3, 64]), ("attn_w_o", [1, 1024, 1024]),
    ("rel_bias", [32, 48]), ("ffn_w_gate_up", [4, 1024, 5632]), ("ffn_w_down", [4, 2816, 1024]),
]


def host_consts():
    c = {}
    c["c_ident"] = np.eye(128, dtype=np.float32)
    k = np.arange(128)
    ang = 2 * np.pi * np.outer(k, k) / 128.0
    c["c_dftc"] = np.concatenate([np.cos(ang), np.sin(ang)], axis=1).astype(ml_dtypes.bfloat16)
    s = np.arange(S)
    prod = (np.outer(s, s) % S).astype(np.float64)
    angs = 2 * np.pi * prod / S
    scale = 1.0 / math.sqrt(S * 128.0)
    cs = (np.cos(angs) * scale)
    sn = (-np.sin(angs) * scale)
    def lay(t):
        o = np.zeros((4, 128, 16, 258), np.float64)
        for nbk in range(4):
            blk = t[:, nbk * 256:nbk * 256 + 257]
            o[nbk, :, :, 0:257] = blk.reshape(16, 128, 257).transpose(1, 0, 2)
        return o
    c["c_seqtab"] = np.ascontiguousarray(np.stack([lay(cs), lay(sn)], 0)).astype(ml_dtypes.bfloat16)
    def t5_bucket(rel):
        half = 16
        max_exact = 8
        n = np.abs(rel)
        sign = (rel > 0).astype(np.int32) * half
        large = max_exact + (np.log(np.maximum(n, 1) / max_exact) / math.log(1024 / max_exact)
                             * (half - max_exact)).astype(np.int32)
        large = np.minimum(large, half - 1)
        return (sign + np.where(n < max_exact, n, large)).astype(np.int32)
    u = np.arange(384)
    off = u - 191
    band = (np.abs(off) <= 64)
    oh = np.zeros((32, 3, 384), np.float32)
    for g, dil in enumerate((1, 4, 16)):
        bk = t5_bucket(off * dil)
        for ui in range(384):
            if band[ui]:
                oh[bk[ui], g, ui] = 1.0
    c["c_oh"] = oh.astype(ml_dtypes.bfloat16)
    c["c_band"] = np.broadcast_to((8.0 * band.astype(np.float32))[None, None, :], (16, 3, 384)).copy()
    c["c_bandneg"] = np.broadcast_to(((band.astype(np.float32) - 1.0) * 30000.0)[None, None, :], (16, 3, 384)).copy()
    c["c_exch"] = np.eye(128, dtype=np.float32)[::-1].copy().astype(ml_dtypes.bfloat16)
    bd = np.zeros((128, 128), np.float32)
    bd[:64, :64] = 1.0
    bd[64:, 64:] = 1.0
    c["c_bd"] = bd.astype(ml_dtypes.bfloat16)
    ex = np.zeros((2, 25), np.float32)
    kk = np.arange(8)
    ex[0, 0:8] = 7 - kk
    ex[0, 8:16] = -1 - kk
    ex[0, 16:24] = kk + 1
    ex[1, 0:8] = kk
    ex[1, 8:16] = kk - 8
    ex[1, 16:24] = 8 - kk
    ex[:, 24] = 1.0
    c["c_s5expo"] = np.broadcast_to(ex[None], (128, 2, 25)).copy()
    jj = np.arange(128) // 16
    mf = (jj[:, None] <= jj[None, :]).astype(np.float32)
    mb = (jj[:, None] >= jj[None, :]).astype(np.float32)
    c["c_s5mask"] = np.stack([np.tile(mf, (1, 4)), np.tile(mb, (1, 4))], 1).copy()
    em = np.ones((128, 4, 2, 2, 128), np.float32)
    em[0:64, 0, 0, 0, :] = 0.0
    em[64:128, 1, 1, 1, :] = 0.0
    em[0:64, 3, :, 0, :] = 0.0
    em[64:128, 3, :, 1, :] = 0.0
    c["c_ebmask"] = em.reshape(128, 4, 512).astype(ml_dtypes.bfloat16)
    return c


CONST_SPECS = [("c_ident", [128, 128], F32), ("c_dftc", [128, 256], BF16),
               ("c_seqtab", [2, 4, 128, 16, 258], BF16), ("c_oh", [32, 3, 384], BF16),
               ("c_band", [16, 3, 384], F32), ("c_bandneg", [16, 3, 384], F32), ("c_exch", [128, 128], BF16), ("c_bd", [128, 128], BF16), ("c_ebmask", [128, 4, 512], BF16),
               ("c_s5expo", [128, 2, 25], F32), ("c_s5mask", [128, 2, 512], F32)]


class K:
    def __init__(self, nc, st):
        self.nc = nc
        self.st = st
        self.P = Prog()
        self.bank = 0
        self.ev = 0

    def sb(self, name, shape, dt):
        return self.st.enter_context(self.nc.sbuf_tensor(name, shape, dt))

    def A(self, off, shape, dt):
        n = int(np.prod(shape[1:]))
        sz = 4 if dt == F32 else 2
        assert off % 4 == 0 and off + n * sz <= ARENA_BYTES, (off, shape)
        v = self.arena[:, off // 2: off // 2 + n * sz // 2]
        if dt == F32:
            v = v.bitcast(F32)
        if len(shape) > 2:
            names = " ".join("d%d" % i for i in range(1, len(shape)))
            kw = {"d%d" % i: shape[i] for i in range(1, len(shape))}
            v = v.rearrange("p (%s) -> p %s" % (names, names), **kw)
        if shape[0] != 128:
            v = v[0:shape[0]]
        return v

    def nb(self):
        b = self.bank
        self.bank = (self.bank + 1) % 8
        return b

    def mm(self, out, lhsT, rhs, start, stop, reads, writes):
        self.P.op("pe", lambda e: e.matmul(out, lhsT, rhs, start=start, stop=stop), reads, writes)

    def tr(self, out, in_, ident, reads, writes):
        self.P.op("pe", lambda e: e.transpose(out, in_, ident), reads, writes)

    def copy(self, eng, out, in_, reads, writes):
        if eng == "act":
            self.P.op("act", lambda e: e.copy(out, in_), reads, writes)
        else:
            self.P.op(eng, lambda e: e.tensor_copy(out, in_), reads, writes)

    def evac_eng(self):
        self.ev += 1
        return "act" if self.ev % 2 else "dve"

    def dma(self, q, out, in_, reads, writes, nonc=False):
        nc = self.nc
        if nonc:
            def f(e):
                with nc.allow_non_contiguous_dma(reason="small strided load"):
                    return e.dma_start(out=out, in_=in_)
        else:
            def f(e):
                return e.dma_start(out=out, in_=in_)
        self.P.op(q, f, reads, writes, dma=True)


def ph_setup(k, dr):
    P = k.P
    k.ident = k.sb("ident", [128, 128], F32)
    k.identb = k.sb("identb", [128, 128], BF16)
    k.ones = k.sb("ones", [128, 128], BF16)
    k.epsb = k.sb("epsb", [128, 1], F32)
    k.gsb = k.sb("gsb", [128, 64], F32)
    k.rstd = k.sb("rstd", [128, 512], F32)
    k.dma("sp", k.ident[:], dr["c_ident"], [], ["ident"])
    P.op("dve", lambda e: e.memset(k.ones[:], 1.0), writes=["ones"])
    P.op("dve", lambda e: e.memset(k.epsb[:], EPS), writes=["epsb"])
    P.op("dve", lambda e: e.tensor_copy(k.identb[:], k.ident[:]), reads=["ident"], writes=["identb"])
    g8 = k.A(65536, [128, 128], F32)
    k.dma("sp", g8[0:32, :], dr["norm_mix_g"].rearrange("l (c p) -> (l c) p", p=128), [], ["g8a"])
    k.dma("sp", g8[32:64, :], dr["norm_ffn_g"].rearrange("l (c p) -> (l c) p", p=128), [], ["g8b"])
    b = k.nb()
    k.tr(k.ps[b][:, 0:64], g8[0:64, :], k.ident[0:64, 0:64], ["g8a", "g8b", "ident"], [("ps", b)])
    k.copy("dve", k.gsb[:], k.ps[b][:, 0:64], [("ps", b)], ["gsb"])


def ph_load_x(k, dr):
    xin = [k.A(i * 4096, [128, 1024], F32) for i in range(2)]
    for t in range(S // 128):
        bq = t % 2
        k.dma("sp", xin[bq], dr["x"][t * 128:(t + 1) * 128, :], [], [("xin", bq)])
        for c0 in (0, 4):
            b = k.nb()
            for c in range(c0, c0 + 4):
                k.tr(k.ps[b][:, (c - c0) * 128:(c - c0 + 1) * 128], xin[bq][:, c * 128:(c + 1) * 128], k.ident[:],
                     [("xin", bq), "ident"], [("ps", b)])
            k.copy(k.evac_eng(), k.xT[:, c0:c0 + 4, t * 128:(t + 1) * 128],
                   k.ps[b][:].rearrange("p (c n) -> p c n", c=4), [("ps", b)],
                   [("xT", c, t // 4) for c in range(c0, c0 + 4)])


def ph_store(k, dr, tiles=None, xo_off=0):
    xo = [k.A(xo_off + i * 4096, [128, 1024], F32) for i in range(2)]
    for t in (range(S // 128) if tiles is None else tiles):
        bq = t % 2
        for c0 in (0, 4):
            b = k.nb()
            for c in range(c0, c0 + 4):
                k.tr(k.ps[b][:, (c - c0) * 128:(c - c0 + 1) * 128], k.xT[:, c, t * 128:(t + 1) * 128], k.ident[:],
                     [("xT", c, t // 4), "ident"], [("ps", b)])
            k.copy(k.evac_eng(), xo[bq][:, c0 * 128:(c0 + 4) * 128], k.ps[b][:], [("ps", b)], [("xo", bq, c0)])
        k.dma("sp", dr["out"][t * 128:(t + 1) * 128, :], xo[bq], [("xo", bq, 0), ("xo", bq, 4)], [("out", t)])


def rmsnorm(k, gcol, hT, hkey, sq, view=None, only_rstd=None):
    P = k.P
    for tb in range(NTB):
        ts = slice(tb * TB, (tb + 1) * TB)
        xk = [("xT", c, tb) for c in range(KC)]
        P.op("act", lambda e, ts=ts: e.activation(sq, k.xT[:, :, ts], AF.Square), reads=xk, writes=["sq"])
        b = k.nb()
        for c in range(KC):
            k.mm(k.ps[b][:], k.ones[:], sq[:, c, :], c == 0, c == KC - 1, ["sq", "ones"], [("ps", b)])
        P.op("act", lambda e, b=b: e.activation(k.rstd[:], k.ps[b][:], AF.Ln, bias=k.epsb[:], scale=1.0 / D),
             reads=[("ps", b), "epsb"], writes=["rstd"])
        P.op("act", lambda e: e.activation(k.rstd[:], k.rstd[:], AF.Exp, scale=-0.5), reads=["rstd"], writes=["rstd"])
        if only_rstd is not None:
            only_rstd(tb, ts)
            continue
        for c in range(KC):
            if view is None:
                o_, x_, r_ = hT[:, c, ts], k.xT[:, c, ts], k.rstd[:]
            else:
                o_, x_, r_ = view(c, tb, ts)
            P.op("dve", lambda e, c=c, o_=o_, x_=x_, r_=r_: e.scalar_tensor_tensor(
                o_, x_, k.gsb[:, gcol + c:gcol + c + 1], r_, ALU.mult, ALU.mult),
                reads=[("xT", c, tb), "gsb", "rstd"], writes=[(hkey, tb)])


def ffn_bufs(k):
    wgu = [k.A(32768 + i * 24576, [128, KC, 2, GMAX * 128], BF16) for i in range(2)]
    wd = [k.A(81920 + i * 12288, [128, GMAX, 1024], BF16) for i in range(2)]
    return wgu, wd


def ffn_load(k, dr, l, gi, bq):
    wgu, wd = ffn_bufs(k)
    f0, gn = FGROUPS[gi]
    for half in range(2):
        src = dr["ffn_w_gate_up"][l][:, half * DFF + f0 * 128: half * DFF + (f0 + gn) * 128].rearrange(
            "(kk p) n -> p kk n", p=128)
        k.dma("pool", wgu[bq][:, :, half, 0:gn * 128], src, [], [("wgu", bq, half)])
    src = dr["ffn_w_down"][l][f0 * 128:(f0 + gn) * 128, :].rearrange("(f p) d -> p f d", p=128)
    k.dma("pool", wd[bq][:, 0:gn, :], src, [], [("wd", bq)])


def ph_ffn(k, dr, l, first_slot=0, preloaded=False, store=False):
    P = k.P
    hT = k.A(0, [128, KC, S], BF16)
    wgu = [k.A(32768 + i * 24576, [128, KC, 2, GMAX * 128], BF16) for i in range(2)]
    wd = [k.A(81920 + i * 12288, [128, GMAX, 1024], BF16) for i in range(2)]
    act = [k.A(106496 + i * 6144, [128, GMAX, TB], BF16) for i in range(2)]
    sg = [k.A(118784 + i * 2048, [128, TB], F32) for i in range(2)]
    sq = k.A(131072, [128, KC, TB], BF16)
    wgu_d = dr["ffn_w_gate_up"]
    wd_d = dr["ffn_w_down"]

    def load(gi):
        ffn_load(k, dr, l, gi, (gi + first_slot) % 2)

    if not preloaded:
        load(0)
    rmsnorm(k, 32 + l * 8, hT, "hT", sq)
    ai = 0
    for gi, (f0, gn) in enumerate(FGROUPS):
        bq = (gi + first_slot) % 2
        if gi + 1 < len(FGROUPS):
            load(gi + 1)
        for tb in range(NTB):
            ts = slice(tb * TB, (tb + 1) * TB)
            ab = ai % 2
            ai += 1
            for fi in range(gn):
                bg = k.nb()
                for c in range(KC):
                    k.mm(k.ps[bg][:], wgu[bq][:, c, 0, fi * 128:(fi + 1) * 128], hT[:, c, ts], c == 0, c == KC - 1,
                         [("wgu", bq, 0), ("hT", tb)], [("ps", bg)])
                bu = k.nb()
                for c in range(KC):
                    k.mm(k.ps[bu][:], wgu[bq][:, c, 1, fi * 128:(fi + 1) * 128], hT[:, c, ts], c == 0, c == KC - 1,
                         [("wgu", bq, 1), ("hT", tb)], [("ps", bu)])
                sb_ = fi % 2
                P.op("act", lambda e, bg=bg, sb_=sb_: e.activation(sg[sb_], k.ps[bg][:], AF.Silu),
                     reads=[("ps", bg)], writes=[("sg", sb_)])
                P.op("dve", lambda e, bu=bu, sb_=sb_, ab=ab, fi=fi: e.tensor_tensor(
                    act[ab][:, fi, :], sg[sb_], k.ps[bu][:], ALU.mult),
                    reads=[("ps", bu), ("sg", sb_)], writes=[("act", ab, fi)])
            for dcn in range(KC):
                bd = k.nb()
                for fi in range(gn):
                    k.mm(k.ps[bd][:], wd[bq][:, fi, dcn * 128:(dcn + 1) * 128], act[ab][:, fi, :], fi == 0, fi == gn - 1,
                         [("wd", bq), ("act", ab, fi)], [("ps", bd)])
                eng = "dve"
                P.op(eng, lambda e, bd=bd, dcn=dcn, ts=ts: e.tensor_tensor(
                    k.xT[:, dcn, ts], k.xT[:, dcn, ts], k.ps[bd][:], ALU.add),
                    reads=[("ps", bd), ("xT", dcn, tb)], writes=[("xT", dcn, tb)])
            if store and gi == len(FGROUPS) - 1 and tb >= 1:
                ph_store(k, dr, tiles=range(4 * (tb - 1), 4 * tb), xo_off=122880)
    if store:
        ph_store(k, dr, tiles=range(12, 16), xo_off=122880)
    P.barrier()


def ph_fnet(k, dr, l, j):
    P = k.P
    hT = k.A(0, [128, KC, S], BF16)
    fT = hT
    Y = k.A(32768, [128, 16, 8, 256], BF16)
    tab = [[k.A(98304 + (i * 2 + w) * 8256, [128, 16, 258], BF16) for w in range(2)] for i in range(2)]
    sq = k.A(131328, [128, KC, TB], BF16)
    tmpa = [k.A(139520 + i * 1040, [128, 260], F32) for i in range(2)]
    wout = k.A(32768, [128, KC, 1024], BF16)
    dftc = k.dftc
    seqtab = dr["c_seqtab"]

    def loadtab(nbk):
        bq = nbk % 2
        for w in range(2):
            k.dma("sp", tab[bq][w], seqtab[w, nbk], [], [("tab", bq, w)])

    loadtab(0)
    rmsnorm(k, l * 8, hT, "hT", sq)
    for st_ in range(16):
        for g0 in range(0, 8, 2):
            b = k.nb()
            for g in (g0, g0 + 1):
                k.mm(k.ps[b][:, (g - g0) * 256:(g - g0 + 1) * 256], hT[:, g, st_ * 128:(st_ + 1) * 128], dftc[:],
                     True, True, [("hT", st_ // 4), "dftc"], [("ps", b)])
            k.copy(k.evac_eng(), Y[:, st_, g0:g0 + 2, :], k.ps[b][:].rearrange("p (g n) -> p g n", g=2),
                   [("ps", b)], [("Y", st_)])
    yk = [("Y", s_) for s_ in range(16)]
    fkeys = [("fT", t_) for t_ in range(NTB)]
    ti = 0
    for nbk in range(4):
        bq = nbk % 2
        if nbk + 1 < 4:
            loadtab(nbk + 1)
        for g in range(8):
            ba = k.nb()
            bb_ = k.nb()
            for w, bnk in ((0, ba), (1, bb_)):
                for st_ in range(16):
                    k.mm(k.ps[bnk][:, 0:257], Y[:, st_, g, w * 128:(w + 1) * 128], tab[bq][w][:, st_, 0:257],
                         st_ == 0, st_ == 15, yk + [("tab", bq, w)], [("ps", bnk)])
            ta = tmpa[ti % 2]
            ti += 1
            P.op("act", lambda e, ta=ta, ba=ba: e.copy(ta[:, 0:257], k.ps[ba][:, 0:257]), reads=[("ps", ba)],
                 writes=[("tmpa", ti % 2)])
            s0 = nbk * 256
            ncol = 257 if nbk == 3 else 256
            P.op("dve", lambda e, ta=ta, bb_=bb_, g=g, s0=s0, ncol=ncol: e.tensor_tensor(
                fT[:, g, s0:s0 + ncol], ta[:, 0:ncol], k.ps[bb_][:, 0:ncol], ALU.add),
                reads=[("ps", bb_), ("tmpa", ti % 2)], writes=fkeys)
            i0 = 1 if nbk == 0 else 0
            i1 = 255 if nbk == 3 else 256
            n_ = i1 - i0 + 1
            base = fT[:, g, S - (s0 + i0):S - (s0 + i0) + 1]
            mv = bass.AP(base.tensor, base.offset, [list(base.ap[0]), [-1, n_]])
            P.op("dve", lambda e, ta=ta, bb_=bb_, mv=mv, i0=i0, i1=i1: e.tensor_tensor(
                mv, ta[:, i0:i1 + 1], k.ps[bb_][:, i0:i1 + 1], ALU.subtract),
                reads=[("ps", bb_), ("tmpa", ti % 2)], writes=fkeys)
    P.barrier()
    k.dma("pool", wout, dr["fnet_w_out"][j].rearrange("(kk p) n -> p kk n", p=128), [], ["wout"])
    ffn_load(k, dr, l, 0, 1)
    for tb in range(NTB):
        ts = slice(tb * TB, (tb + 1) * TB)
        for dcn in range(KC):
            b = k.nb()
            for c in range(KC):
                k.mm(k.ps[b][:], wout[:, c, dcn * 128:(dcn + 1) * 128], fT[:, c, ts], c == 0, c == KC - 1,
                     ["wout", ("fT", tb)], [("ps", b)])
            P.op("dve", lambda e, b=b, dcn=dcn, ts=ts: e.tensor_tensor(
                k.xT[:, dcn, ts], k.xT[:, dcn, ts], k.ps[b][:], ALU.add),
                reads=[("ps", b), ("xT", dcn, tb)], writes=[("xT", dcn, tb)])
    P.barrier()


PAD = 64
CMW = S + 2 * PAD
DILS = (1, 4, 16)


def ph_attn(k, dr, l, j):
    P = k.P
    nc = k.nc
    hT = k.A(0, [128, KC, S], BF16)
    oT = k.A(32768, [128, KC, S], BF16)
    wqkv = k.A(65536, [128, KC, 384], BF16)
    cm = [[k.A(71680 + (i * 3 + w) * 4352, [128, CMW], BF16) for w in range(3)] for i in range(2)]
    vaug = [k.A(97792 + i * 4420, [128, 17, 2, 65], BF16) for i in range(2)]
    acc = [k.A(106632 + i * 8192, [128, S], F32) for i in range(2)]
    EB = [[k.A(123016 + (hh * 3 + v) * 1024, [128, 2, 256], BF16) for v in range(3)] for hh in range(2)]
    Et = [k.A(129160 + i * 1024, [128, 512], BF16) for i in range(2)]
    ebtmp = k.A(129160, [128, 512], F32)
    Pt = [k.A(131208 + i * 1024, [128, 512], BF16) for i in range(2)]
    hank = [k.A(133256 + i * 256, [128, 128], BF16) for i in range(4)]
    gtile = k.A(134280, [128, 128], F32)
    relbh = k.A(134792, [128, 48], BF16)
    relbl = k.A(134888, [128, 48], BF16)
    sqh = k.A(135304, [128, 512], BF16)
    ebm = k.A(137672, [128, 4, 512], BF16)
    exch = k.A(136328, [128, 128], BF16)
    bd = k.A(136584, [128, 128], BF16)
    qkg = k.A(136840, [128, 8], F32)
    sq = k.A(65536, [128, KC, TB], BF16)
    scr = k.attn_scr

    rmsnorm(k, l * 8, hT, "hT", sq)
    P.barrier()
    k.dma("sp", exch, dr["c_exch"], [], ["exch"])
    k.dma("sp", bd, dr["c_bd"], [], ["bd"])
    k.dma("sp", ebm, dr["c_ebmask"], [], ["ebm"])
    for i in range(2):
        for w in range(3):
            P.op("dve", lambda e, i=i, w=w: e.memset(cm[i][w], 0.0), writes=[("cm", i, w)])
        P.op("dve", lambda e, i=i: e.memset(vaug[i], 1.0), writes=[("vaug", i)])
    gt = gtile
    for w, nm in enumerate(("attn_q_gain", "attn_k_gain")):
        for hh in range(2):
            k.dma("sp", gt[w * 3:(w + 1) * 3, hh * 64:(hh + 1) * 64], dr[nm][j], [], [("gt", w, hh)])
    b = k.nb()
    k.tr(k.ps[b][:, 0:6], gt[0:6, :], k.ident[0:6, 0:6], [("gt", w, hh) for w in range(2) for hh in range(2)] + ["ident"],
         [("ps", b)])
    k.copy("dve", qkg[:, 0:6], k.ps[b][:, 0:6], [("ps", b)], ["qkg"])
    relb = k.A(106632 + 3 * 4608, [128, 48], F32)
    ohs = k.A(106632, [128, 3, 384], BF16)
    k.dma("sp", relb[0:32, :], dr["rel_bias"], [], ["relb"])
    k.dma("sp", ohs[0:32], dr["c_oh"], [], ["ohs"])
    bandt = k.A(106632 + 4608, [128, 3, 384], F32)
    earr = k.A(106632 + 4608 + 4608, [128, 3, 384], F32)
    earb = k.A(106632 + 2304, [128, 3, 384], BF16)
    k.dma("sp", bandt[0:16], dr["c_band"], [], ["bandt"])
    P.op("dve", lambda e: e.tensor_copy(relbh[0:32, :], relb[0:32, :]), reads=["relb"], writes=["relbh"])
    P.op("dve", lambda e: e.tensor_tensor(relb[0:32, :], relb[0:32, :], relbh[0:32, :], ALU.subtract),
         reads=["relb", "relbh"], writes=["relb"])
    P.op("dve", lambda e: e.tensor_copy(relbl[0:32, :], relb[0:32, :]), reads=["relb"], writes=["relbl"])
    bandn = k.A(106632 + 13824 + 256, [128, 3, 384], F32)
    k.dma("sp", bandn[0:16], dr["c_bandneg"], [], ["bandn"])
    bb = [k.nb() for g in range(3)]
    for g in range(3):
        k.mm(k.ps[bb[g]][0:16, 0:384], relbh[0:32, 16 * g:16 * g + 16], ohs[0:32, g, :], True, False,
             ["relbh", "ohs"], [("ps", bb[g])])
        k.mm(k.ps[bb[g]][0:16, 0:384], relbl[0:32, 16 * g:16 * g + 16], ohs[0:32, g, :], False, True,
             ["relbl", "ohs"], [("ps", bb[g])])
        P.op("dve", lambda e, g=g: e.tensor_tensor(earr[0:16, g, :], k.ps[bb[g]][0:16, 0:384], bandt[0:16, g, :], ALU.mult),
             reads=[("ps", bb[g]), "bandt"], writes=["earr"])
    P.op("dve", lambda e: e.tensor_tensor(earb[0:16], earr[0:16], bandn[0:16], ALU.add), reads=["earr", "bandn"],
         writes=["earb"])
    k.dma("sp", scr.ap(), earb[0:16], ["earb"], ["scr"])
    P.barrier()

    wq = dr["attn_w_qkv"][j]
    it = 0
    import os
    STOP = float(os.environ.get("ATTN_STOP", "99"))
    for hp in range(8 if STOP > 10 else (1 if STOP > 0 else 0)):
        for g in range(3):
            d = DILS[g]
            seg = S // d
            bq = it % 2
            it += 1
            for w in range(3):
                c0 = g * 3072 + w * 1024 + hp * 128
                k.dma("pool", wqkv[:, :, w * 128:(w + 1) * 128],
                      wq[:, c0:c0 + 128].rearrange("(kk p) n -> p kk n", p=128), [], [("wqkv", w)])
            if STOP <= 0.2:
                continue
            for hh in range(2):
                row = (2 * hp + hh) * 3 + g
                for ab in range(2):
                    k.dma("sp", hank[hh * 2 + ab], bass.AP(scr, row * 384 + ab * 128, [[1, 128], [1, 128]]),
                          ["scr"], [("hank", hh * 2 + ab)])
            bt = k.nb()
            for i in range(4):
                k.mm(k.ps[bt][:, i * 128:(i + 1) * 128], hank[i], exch, True, True, [("hank", i), "exch"], [("ps", bt)])
            nvar = (3, 2, 1)[g]
            if STOP <= 0.4:
                continue
            for hh in range(2):
                src = k.ps[bt][:, hh * 256:(hh + 1) * 256].unsqueeze(1).to_broadcast([128, 2, 256])
                for v in range(nvar):
                    if v == 2:
                        P.op("dve", lambda e, hh=hh, v=v, src=src: e.tensor_copy(EB[hh][v], src),
                             reads=[("ps", bt)], writes=[("EB", hh, v)])
                        continue
                    mv = 3 if g == 2 else v
                    mk = ebm[:, mv, :].rearrange("p (a b) -> p a b", a=2)
                    ebt3 = ebtmp.rearrange("p (a b) -> p a b", a=2)
                    P.op("dve", lambda e, src=src, mk=mk, ebt3=ebt3: e.scalar_tensor_tensor(
                        ebt3, src, 30000.0, mk, ALU.add, ALU.mult), reads=[("ps", bt), "ebm", "ebtmp"], writes=["ebtmp", ("Et", 0), ("Et", 1)])
                    P.op("dve", lambda e, hh=hh, v=v, ebt3=ebt3: e.tensor_scalar(EB[hh][v], ebt3, -30000.0, None, ALU.add),
                         reads=["ebtmp"], writes=[("EB", hh, v)])
            if STOP <= 0.6:
                continue
            sqhs = [sqh, k.A(141768, [128, 512], BF16)]
            rstds = [(k.rstd[:], ["rstd"]), (ebtmp, ["ebtmp", ("Et", 0), ("Et", 1)])]
            pend = []

            def stageB(rec):
                w, tb, b, ov, iv, ni = rec
                sq_ = sqhs[ni % 2]
                rs_, rkeys = rstds[ni % 2]
                b2 = k.nb()
                k.mm(k.ps[b2][:], bd, sq_, True, True, ["bd", ("sqh", ni % 2)], [("ps", b2)])
                P.op("act", lambda e: e.activation(rs_, k.ps[b2][:], AF.Ln, bias=k.epsb[:], scale=1.0 / 64),
                     reads=[("ps", b2), "epsb"] + rkeys, writes=rkeys)
                P.op("act", lambda e: e.activation(rs_, rs_, AF.Exp, scale=-0.5), reads=rkeys, writes=rkeys)
                rv = rs_.rearrange("p (m r) -> p r m", r=d)
                gc = w * 3 + g
                P.op("dve", lambda e: e.scalar_tensor_tensor(ov, iv, qkg[:, gc:gc + 1], rv, ALU.mult, ALU.mult),
                     reads=[("ps", b), "qkg"] + rkeys, writes=[("cm", bq, w)])

            ni = 0
            for w in (2, 0, 1):
                buf = cm[bq][w]
                for tb in range(NTB):
                    ts = slice(tb * TB, (tb + 1) * TB)
                    b = k.nb()
                    for c in range(KC):
                        k.mm(k.ps[b][:], wqkv[:, c, w * 128:(w + 1) * 128], hT[:, c, ts], c == 0, c == KC - 1,
                             [("wqkv", w), ("hT", tb)], [("ps", b)])
                    cnt = TB // d
                    m0 = tb * cnt
                    ov = buf[:, PAD:PAD + S].rearrange("p (r m) -> p r m", r=d)[:, :, m0:m0 + cnt]
                    iv = k.ps[b][:].rearrange("p (m r) -> p r m", r=d)
                    if w == 2:
                        P.op("act", lambda e, ov=ov, iv=iv: e.copy(ov, iv), reads=[("ps", b)], writes=[("cm", bq, w)])
                    else:
                        sq_ = sqhs[ni % 2]
                        P.op("act", lambda e, b=b, sq_=sq_: e.activation(sq_, k.ps[b][:], AF.Square), reads=[("ps", b)],
                             writes=[("sqh", ni % 2)])
                        pend.append((w, tb, b, ov, iv, ni))
                        ni += 1
                        if len(pend) == 2:
                            stageB(pend.pop(0))
            while pend:
                stageB(pend.pop(0))
            qc, kc_, vc = cm[bq]
            if STOP <= 1:
                continue
            for t0 in (0, 8, 16):
                nt = min(8, 17 - t0)
                b = k.nb()
                pv = k.ps[b][:].bitcast(BF16)
                for t in range(t0, t0 + nt):
                    k.tr(pv[:, (t - t0) * 128:(t - t0 + 1) * 128], vc[:, t * 128:(t + 1) * 128], k.identb[:],
                         [("cm", bq, 2), "identb"], [("ps", b)])
                k.copy(k.evac_eng(), vaug[bq][:, t0:t0 + nt, :, 0:64],
                       pv[:, 0:nt * 128].rearrange("p (t h e) -> p t h e", t=nt, h=2), [("ps", b)], [("vaug", bq)])
            if STOP <= 2:
                continue
            steps = [(hh, qp) for hh in range(2) for qp in range(8)]

            def emit_scores(hh, qp):
                hs = slice(64 * hh, 64 * hh + 64)
                bs = k.nb()
                if g == 0:
                    var = 0 if qp == 0 else (1 if qp == 7 else 2)
                elif g == 1:
                    var = 0 if qp % 2 == 0 else 1
                else:
                    var = 0
                ebv = EB[hh][var].rearrange("p a b -> p (a b)")
                k.mm(k.ps[bs][:], k.identb[:], ebv, True, False, ["identb", ("EB", hh, var)], [("ps", bs)])
                for qi in range(2):
                    qb = 2 * qp + qi
                    for ab in range(2):
                        k.mm(k.ps[bs][:, (qi * 2 + ab) * 128:(qi * 2 + ab + 1) * 128],
                             kc_[hs, 128 * (qb + ab):128 * (qb + ab + 1)], qc[hs, PAD + 128 * qb:PAD + 128 * (qb + 1)],
                             False, (qi == 1 and ab == 1), [("cm", bq, 0), ("cm", bq, 1)], [("ps", bs)])
                return bs

            LOOK = 3
            sbank = {}
            for i in range(LOOK):
                sbank[i] = emit_scores(*steps[i])
            bo = None
            for i, (hh, qp) in enumerate(steps):
                pb = i % 2
                bs = sbank.pop(i)
                P.op("act", lambda e, bs=bs, pb=pb: e.activation(Pt[pb], k.ps[bs][:], AF.Exp, scale=0.125),
                     reads=[("ps", bs)], writes=[("Pt", pb)])
                if i + LOOK < len(steps):
                    sbank[i + LOOK] = emit_scores(*steps[i + LOOK])
                if qp % 2 == 0:
                    bo = k.nb()
                for qi in range(2):
                    qb = 2 * qp + qi
                    slot = (qp % 2) * 2 + qi
                    for ab in range(2):
                        k.mm(k.ps[bo][0:65, slot * 128:(slot + 1) * 128], vaug[bq][:, qb + ab, hh, :],
                             Pt[pb][:, (qi * 2 + ab) * 128:(qi * 2 + ab + 1) * 128], ab == 0, ab == 1,
                             [("vaug", bq), ("Pt", pb)], [("ps", bo)])
                if qp % 2 == 1:
                    qq = qp // 2
                    a_ = acc[hh][0:65, :]
                    pin = k.ps[bo][0:65, :]
                    if d == 1:
                        av = a_[:, 512 * qq:512 * (qq + 1)]
                    elif d == 4:
                        av = a_.rearrange("p (m r) -> p m r", r=4)[:, :, qq]
                    else:
                        av = a_.rearrange("p (m r) -> p r m", r=16)[:, 4 * qq:4 * qq + 4, :]
                        pin = pin.rearrange("p (r m) -> p r m", r=4)
                    if g == 0:
                        P.op("act", lambda e, av=av, pin=pin: e.copy(av, pin), reads=[("ps", bo)], writes=[("acc", hh)])
                    else:
                        P.op("dve", lambda e, av=av, pin=pin: e.tensor_tensor(av, av, pin, ALU.add),
                             reads=[("ps", bo), ("acc", hh)], writes=[("acc", hh)])
        if hp == 7:
            wo_pre = k.A(65536, [128, KC, 1024], BF16)
            k.dma("pool", wo_pre, dr["attn_w_o"][j].rearrange("(kk p) n -> p kk n", p=128), [],
                  ["wo"] + [("wqkv", w_) for w_ in range(3)] + [("cm", 0, w_) for w_ in range(3)])
        for hh in range(2 if STOP > 3 else 0):
            P.op("act", lambda e, hh=hh: e.activation(acc[hh][64:65, :], acc[hh][64:65, :], AF.Ln),
                 reads=[("acc", hh)], writes=[("acc", hh)])
            P.op("act", lambda e, hh=hh: e.activation(acc[hh][64:65, :], acc[hh][64:65, :], AF.Exp, scale=-1.0),
                 reads=[("acc", hh)], writes=[("acc", hh)])
            for tb in range(NTB):
                ts = slice(tb * TB, (tb + 1) * TB)
                rh = Et[tb % 2]
                rl = Pt[tb % 2]
                P.op("dve", lambda e, rh=rh, hh=hh, ts=ts: e.tensor_copy(rh[64:65, :], acc[hh][64:65, ts]),
                     reads=[("acc", hh)], writes=[("Et", tb % 2), "ebtmp"])
                P.op("dve", lambda e, rh=rh, hh=hh, ts=ts: e.tensor_tensor(
                    k.rstd[64:65, :], acc[hh][64:65, ts], rh[64:65, :], ALU.subtract),
                    reads=[("acc", hh), ("Et", tb % 2)], writes=["rstd"])
                P.op("dve", lambda e, rl=rl: e.tensor_copy(rl[64:65, :], k.rstd[64:65, :]),
                     reads=["rstd"], writes=[("Pt", tb % 2)])
                b = k.nb()
                k.mm(k.ps[b][0:64, :], k.ones[64:65, 0:64], rh[64:65, :], True, False,
                     ["ones", ("Et", tb % 2)], [("ps", b)])
                k.mm(k.ps[b][0:64, :], k.ones[64:65, 0:64], rl[64:65, :], False, True,
                     ["ones", ("Pt", tb % 2)], [("ps", b)])
                P.op("dve", lambda e, hh=hh, hp=hp, ts=ts, b=b: e.tensor_tensor(
                    oT[64 * hh:64 * hh + 64, hp, ts], acc[hh][0:64, ts], k.ps[b][0:64, :], ALU.mult),
                    reads=[("ps", b), ("acc", hh)], writes=[("oT", tb)])
    P.barrier()
    wo = k.A(65536, [128, KC, 1024], BF16)
    for tb in range(NTB):
        ts = slice(tb * TB, (tb + 1) * TB)
        for dcn in range(KC):
            b = k.nb()
            for c in range(KC):
                k.mm(k.ps[b][:], wo[:, c, dcn * 128:(dcn + 1) * 128], oT[:, c, ts], c == 0, c == KC - 1,
                     ["wo", ("oT", tb)], [("ps", b)])
            P.op("dve", lambda e, b=b, dcn=dcn, ts=ts: e.tensor_tensor(
                k.xT[:, dcn, ts], k.xT[:, dcn, ts], k.ps[b][:], ALU.add),
                reads=[("ps", b), ("xT", dcn, tb)], writes=[("xT", dcn, tb)])
    P.barrier()


TWO_PI = 2.0 * math.pi
MAGIC = 12582912.0


def ph_s5(k, dr, l, j):
    P = k.P
    nc = k.nc
    Dh, D2, Vd, Td = k.s5_dh, k.s5_d2, k.s5_vd, k.s5_td
    cnt = [0]

    def uid(s_):
        cnt[0] += 1
        return "%s%d" % (s_, cnt[0])

    def tt(eng, out, a, b, op, reads, writes):
        P.op(eng, lambda e: e.tensor_tensor(out, a, b, op), reads, writes)

    hTp = k.A(0, [128, KC, 8, 256], BF16)
    sq = k.A(32768, [128, KC, TB], BF16)

    def view(c, tb, ts):
        return (hTp[:, c, :, 64 * tb:64 * tb + 64],
                k.xT[:, c, ts].rearrange("p (c j) -> p j c", j=8),
                k.rstd[:].rearrange("p (c j) -> p j c", j=8))

    rmsnorm(k, l * 8, None, "hTp", sq, view=view)
    hk = [("hTp", tb) for tb in range(NTB)]
    for kc in range(KC):
        for g8 in range(8):
            g = kc * 8 + g8
            k.dma("sp" if g8 % 2 == 0 else "act", Dh.ap()[g].rearrange("(j p) c -> p j c", p=16),
                  hTp[g8 * 16:(g8 + 1) * 16, kc, :, :], hk, [("Dh", g)])
    P.barrier()

    import os
    S5STOP = float(os.environ.get("S5_STOP", "99"))
    if S5STOP <= 0:
        return
    BIG = [128, 2, 32, 25]
    def big(i):
        return k.A(i * 6400, BIG, F32)
    AR, AI, KK, XR, SN, CS = [big(i) for i in range(6)]
    CRE, CIM = big(6), big(7)
    o0 = 8 * 6400
    lamt = k.A(o0, [128, 2, 2, 64], F32)
    LAM = k.A(o0 + 1024, [128, 2, 2, 32], F32)
    LDT = k.A(o0 + 1536, [128, 2, 32], F32)
    RDT = k.A(o0 + 1792, [128, 2, 32], F32)
    IDT = k.A(o0 + 2048, [128, 2, 32], F32)
    EXPO = k.A(o0 + 2304, [128, 2, 25], F32)
    FRE = k.A(o0 + 2504, [128, 2, 32], F32)
    FIM = k.A(o0 + 2760, [128, 2, 32], F32)
    DEN = k.A(o0 + 3016, [128, 2, 32], F32)
    TA = k.A(o0 + 3272, [128, 2, 32], F32)
    TB_ = k.A(o0 + 3528, [128, 2, 32], F32)
    MSK = k.A(o0 + 3784, [128, 2, 512], F32)
    o1 = o0 + 3784 + 4096
    ACO = [[k.A(139264 + (w * 2 + d) * 256, [128, 2, 32], F32) for w in range(2)] for d in range(2)]
    A1_all = k.A(139264, [128, 2, 2, 32], F32)
    A2_all = k.A(139776, [128, 2, 2, 32], F32)

    k.dma("sp", EXPO, dr["c_s5expo"], [], ["EXPO"])
    k.dma("sp", MSK, dr["c_s5mask"], [], ["MSK"])
    for w, nm in enumerate(("s5_lambda_re", "s5_lambda_im")):
        k.dma("sp", lamt[0:64, w], dr[nm][j].rearrange("d g n -> g d n"), [], [("lamt", w)])
    for gh in range(2):
        k.dma("sp", LDT[gh * 64:(gh + 1) * 64], dr["s5_log_dt"][j][:, gh * 32:(gh + 1) * 32].partition_broadcast(64),
              [], [("LDT", gh)])
    bl = [k.nb(), k.nb()]
    for gh in range(2):
        for w in range(2):
            for d in range(2):
                col = (w * 2 + d) * 32
                k.tr(k.ps[bl[gh]][0:64, col:col + 32], lamt[gh * 32:(gh + 1) * 32, w, d, :],
                     k.ident[gh * 32:(gh + 1) * 32, gh * 32:(gh + 1) * 32], [("lamt", w), "ident"], [("ps", bl[gh])])
        k.copy("dve", LAM[gh * 64:(gh + 1) * 64].rearrange("p w d g -> p (w d g)"),
               k.ps[bl[gh]][0:64, 0:128], [("ps", bl[gh])], [("LAM", gh)])
    P.op("act", lambda e: e.activation(LDT, LDT, AF.Exp), reads=[("LDT", 0), ("LDT", 1)], writes=["DT"])
    LK = [("LAM", 0), ("LAM", 1)]
    tt("dve", RDT, LAM[:, 0], LDT, ALU.mult, LK + ["DT"], ["RDT"])
    tt("dve", IDT, LAM[:, 1], LDT, ALU.mult, LK + ["DT"], ["IDT"])
    ex_b = EXPO.unsqueeze(2).to_broadcast(BIG)
    tt("dve", AR, RDT.unsqueeze(3).to_broadcast(BIG), ex_b, ALU.mult, ["RDT", "EXPO"], ["AR"])
    tt("dve", AI, IDT.unsqueeze(3).to_broadcast(BIG), ex_b, ALU.mult, ["IDT", "EXPO"], ["AI"])
    P.op("act", lambda e: e.activation(AR, AR, AF.Exp), reads=["AR"], writes=["AR"])

    if S5STOP <= 0.1:
        P.barrier()
        return

    def sinred(out, shift, okey):
        P.op("dve", lambda e: e.tensor_scalar(KK, AI, shift, 1.0 / TWO_PI, ALU.add, ALU.mult),
             reads=["AI"], writes=["KK"])
        P.op("dve", lambda e: e.tensor_scalar(KK, KK, MAGIC, None, ALU.add), reads=["KK"], writes=["KK"])
        P.op("dve", lambda e: e.tensor_scalar(KK, KK, MAGIC, None, ALU.subtract), reads=["KK"], writes=["KK"])
        P.op("dve", lambda e: e.scalar_tensor_tensor(XR, KK, -TWO_PI, AI, ALU.mult, ALU.add), reads=["KK", "AI"],
             writes=["XR"])
        P.op("dve", lambda e: e.tensor_scalar(XR, XR, shift, 3.1415925, ALU.add, ALU.min), reads=["XR"], writes=["XR"])
        P.op("dve", lambda e: e.tensor_scalar(XR, XR, -3.1415925, None, ALU.max), reads=["XR"], writes=["XR"])
        P.op("act", lambda e: e.activation(out, XR, AF.Sin), reads=["XR"], writes=[okey])

    sinred(SN, 0.0, "SN")
    sinred(CS, math.pi / 2.0, "CS")
    if S5STOP <= 0.2:
        P.barrier()
        return
    tt("dve", CRE, AR, CS, ALU.mult, ["AR", "CS"], ["CRE"])
    tt("dve", CIM, AR, SN, ALU.mult, ["AR", "SN"], ["CIM"])
    LR, LI = LAM[:, 0], LAM[:, 1]
    P1R, P1I = CRE[:, :, :, 24], CIM[:, :, :, 24]
    P.op("dve", lambda e: e.tensor_scalar(TA, P1R, -1.0, None, ALU.add), reads=["CRE"], writes=["TA"])
    tt("dve", DEN, LR, LR, ALU.mult, LK, ["DEN"])
    tt("dve", TB_, LI, LI, ALU.mult, LK, ["TB"])
    tt("dve", DEN, DEN, TB_, ALU.add, ["DEN", "TB"], ["DEN"])
    P.op("dve", lambda e: e.reciprocal(DEN, DEN), reads=["DEN"], writes=["DEN"])
    tt("dve", FRE, TA, LR, ALU.mult, ["TA"] + LK, ["FRE"])
    tt("dve", TB_, P1I, LI, ALU.mult, ["CIM"] + LK, ["TB"])
    tt("dve", FRE, FRE, TB_, ALU.add, ["FRE", "TB"], ["FRE"])
    tt("dve", FRE, FRE, DEN, ALU.mult, ["FRE", "DEN"], ["FRE"])
    tt("dve", FIM, P1I, LR, ALU.mult, ["CIM"] + LK, ["FIM"])
    tt("dve", TB_, TA, LI, ALU.mult, ["TA"] + LK, ["TB"])
    tt("dve", FIM, FIM, TB_, ALU.subtract, ["FIM", "TB"], ["FIM"])
    tt("dve", FIM, FIM, DEN, ALU.mult, ["FIM", "DEN"], ["FIM"])
    for d in range(2):
        idx = 23 if d == 0 else 16
        a1, a2 = ACO[d]
        for h_ in range(2):
            P.op("dve", lambda e, a1=a1, h_=h_, d=d, idx=idx: e.tensor_copy(a1[:, h_, :], CRE[:, d, :, idx]),
                 reads=["CRE"], writes=[("ACO", d, 0)])
        P.op("dve", lambda e, a2=a2, d=d, idx=idx: e.tensor_scalar(a2[:, 0, :], CIM[:, d, :, idx], -1.0, None, ALU.mult),
             reads=["CIM"], writes=[("ACO", d, 1)])
        P.op("dve", lambda e, a2=a2, d=d, idx=idx: e.tensor_copy(a2[:, 1, :], CIM[:, d, :, idx]),
             reads=["CIM"], writes=[("ACO", d, 1)])
    S16 = [128, 2, 32, 16]
    fr_b = FRE.unsqueeze(3).to_broadcast(S16)
    fi_b = FIM.unsqueeze(3).to_broadcast(S16)
    q_r, q_i = CRE[:, :, :, 0:16], CIM[:, :, :, 0:16]
    t_a, t_b = AR[:, :, :, 0:16], AI[:, :, :, 0:16]
    tt("dve", t_a, q_r, fi_b, ALU.mult, ["CRE", "FIM", "AR"], ["AR"])
    tt("dve", t_b, q_i, fi_b, ALU.mult, ["CIM", "FIM", "AI"], ["AI"])
    tt("dve", q_r, q_r, fr_b, ALU.mult, ["CRE", "FRE"], ["CRE"])
    tt("dve", q_i, q_i, fr_b, ALU.mult, ["CIM", "FRE"], ["CIM"])
    tt("dve", q_r, q_r, t_b, ALU.subtract, ["CRE", "AI"], ["CRE"])
    tt("dve", q_i, q_i, t_a, ALU.add, ["CIM", "AR"], ["CIM"])
    if S5STOP <= 0.3:
        P.barrier()
        return
    P.barrier()
    B_sb = k.A(0, [128, 2, 2, 32, 16], F32)
    C_sb = k.A(8192, [128, 2, 2, 32, 16], F32)
    C_nat = k.A(16384, [128, 2, 2, 8, 64], F32)
    for d in range(2):
        for ri, nm in enumerate(("s5_b_re", "s5_b_im")):
            for gh in range(2):
                k.dma("sp", B_sb[gh * 64:(gh + 1) * 64, d, ri], dr[nm][j][d][gh * 32:(gh + 1) * 32].rearrange("g n p -> n g p"),
                      [], [("B_sb", d, ri, gh)])
        for ri, nm in enumerate(("s5_c_re", "s5_c_im")):
            k.dma("sp", C_nat[:, d, ri], dr[nm][j][d].rearrange("(gb g8) po n -> (g8 po) gb n", g8=8), [], [("C_nat", d, ri)])
    for d in range(2):
        for ri in range(2):
            for gh in range(2):
                b = k.nb()
                for q4 in range(4):
                    gb = gh * 4 + q4
                    k.tr(k.ps[b][0:64, q4 * 128:(q4 + 1) * 128], C_nat[:, d, ri, gb, :], k.ident[:],
                         [("C_nat", d, ri), "ident"], [("ps", b)])
                k.copy(k.evac_eng(), C_sb[gh * 64:(gh + 1) * 64, d, ri].rearrange("p g q -> p (g q)"),
                       k.ps[b][0:64, :], [("ps", b)], [("C_sb", d, ri, gh)])
    bkeys = [("B_sb", d, ri, gh) for d in range(2) for ri in range(2) for gh in range(2)]
    ckeys = [("C_sb", d, ri, gh) for d in range(2) for ri in range(2) for gh in range(2)]
    if S5STOP <= 0.6:
        P.barrier()
        return
    W1 = [k.A(65536 + d * 16384, [128, 64, 128], BF16) for d in range(2)]
    Q4 = [128, 8, 8, 16]
    t1 = k.A(24576, Q4, F32)
    t2 = k.A(28672, Q4, F32)
    TQ = k.A(32768, [128, 2, 8, 128], BF16)
    WQ1 = k.A(98304, [128, 2, 2, 8, 128], BF16)
    WQ2 = k.A(106496, [128, 2, 2, 8, 128], BF16)
    VQ = k.A(114688, [128, 2, 2, 8, 128], BF16)

    def cplx(out_re, out_im, d, slot, X, q, neg_im, xkeys, okey):
        qs = slice(q * 8, q * 8 + 8)
        cr = CRE[:, d, qs, slot * 8:slot * 8 + 8].unsqueeze(3).to_broadcast(Q4)
        ci = CIM[:, d, qs, slot * 8:slot * 8 + 8].unsqueeze(3).to_broadcast(Q4)
        xr = X[:, d, 0, qs, :].unsqueeze(2).to_broadcast(Q4)
        xi = X[:, d, 1, qs, :].unsqueeze(2).to_broadcast(Q4)
        o_r = out_re.rearrange("p g (a b) -> p g a b", a=8)
        o_i = out_im.rearrange("p g (a b) -> p g a b", a=8)
        rd = ["CRE", "CIM"] + xkeys
        tt("dve", t1, cr, xr, ALU.mult, rd + ["t1"], ["t1"])
        tt("dve", t2, ci, xi, ALU.mult, rd + ["t2"], ["t2"])
        tt("dve", o_r, t1, t2, ALU.subtract, ["t1", "t2"], [okey])
        tt("dve", t1, cr, xi, ALU.mult, rd + ["t1"], ["t1"])
        tt("dve", t2, ci, xr, ALU.mult, rd + ["t2"], ["t2"])
        if neg_im:
            P.op("dve", lambda e: e.scalar_tensor_tensor(o_i, t1, -1.0, t2, ALU.mult, ALU.subtract),
                 reads=["t1", "t2"], writes=[okey])
        else:
            tt("dve", o_i, t1, t2, ALU.add, ["t1", "t2"], [okey])

    for q in range(4):
        for d in range(2):
            cplx(WQ1[:, d, 0], WQ1[:, d, 1], d, 0, B_sb, q, False, bkeys, ("WQ1", d))
            cplx(WQ2[:, d, 0], WQ2[:, d, 1], d, 1, B_sb, q, False, bkeys, ("WQ2", d))
            cplx(VQ[:, d, 0], VQ[:, d, 1], d, 2, C_sb, q, True, ckeys, ("VQ", d))
        for d in range(2):
            for gh in range(2):
                b = k.nb()
                pv = k.ps[b][:].bitcast(BF16)
                for g8 in range(8):
                    for ri in range(2):
                        col = (g8 * 2 + ri) * 64
                        k.tr(pv[:, col:col + 64], WQ1[gh * 64:(gh + 1) * 64, d, ri, g8, :],
                             k.identb[gh * 64:(gh + 1) * 64, gh * 64:(gh + 1) * 64], [("WQ1", d), "identb"], [("ps", b)])
                g0 = gh * 32 + q * 8
                k.copy(k.evac_eng(), W1[d][:, g0:g0 + 8, :], pv.rearrange("p (g c) -> p g c", g=8), [("ps", b)],
                       [("W1", d)])
        for d in range(2):
            for gh in range(2):
                g0 = gh * 32 + q * 8
                for ri in range(2):
                    k.dma("sp", Vd.ap()[d, g0:g0 + 8].rearrange("g (ri n) q -> n ri g q", ri=2)[:, ri],
                          VQ[gh * 64:(gh + 1) * 64, d, ri], [("VQ", d)], [("Vd", d, gh, q, ri)])
        for gh in range(2):
            hs = slice(gh * 64, (gh + 1) * 64)
            for g4 in range(2):
                bf_ = k.nb()
                bb_ = k.nb()
                for d, bnk in ((0, bf_), (1, bb_)):
                    for gi in range(4):
                        g8 = g4 * 4 + gi
                        for ri in range(2):
                            k.mm(k.ps[bnk][:, gi * 128:(gi + 1) * 128], WQ2[hs, d, ri, g8, :], VQ[hs, d, ri, g8, :],
                                 ri == 0, ri == 1, [("WQ2", d), ("VQ", d)], [("ps", bnk)])
                tf = k.A(122880, [128, 512], F32)
                tb_ = k.A(124928, [128, 512], F32)
                tt("dve", tf, k.ps[bf_][:], MSK[:, 0, :], ALU.mult, [("ps", bf_), "MSK", "tf"], ["tf"])
                tt("dve", tb_, k.ps[bb_][:], MSK[:, 1, :], ALU.mult, [("ps", bb_), "MSK", "tb_"], ["tb_"])
                tt("dve", TQ[:, gh, g4 * 4:g4 * 4 + 4, :].rearrange("p g c -> p (g c)"), tf, tb_, ALU.add,
                   ["tf", "tb_"], [("TQ", gh)])
            g0 = gh * 32 + q * 8
            k.dma("sp", Td.ap()[g0:g0 + 8].rearrange("g a b -> a g b"), TQ[:, gh], [("TQ", gh)], [("Td", gh, q)])
    P.barrier()

    if S5STOP <= 1:
        return
    S_all = k.A(0, [128, 2, 2, 32, 256], BF16)
    Ust = [[k.A(98304 + (i * 2 + gh) * 4096, [128, 8, 256], BF16) for gh in range(2)] for i in range(2)]
    for pb in range(4):
        bq = pb % 2
        for gh in range(2):
            gb = pb + 4 * gh
            k.dma("sp", Ust[bq][gh], Dh.ap()[gb * 8:(gb + 1) * 8].rearrange("g a c -> a g c"), [], [("Ust", bq, gh)])
        for g8 in range(8):
            g32 = pb * 8 + g8
            for d in range(2):
                b = k.nb()
                for gh in range(2):
                    g = gh * 32 + g32
                    for ri in range(2):
                        k.mm(k.ps[b][gh * 64:(gh + 1) * 64, ri * 256:(ri + 1) * 256], W1[d][:, g, ri * 64:(ri + 1) * 64],
                             Ust[bq][gh][:, g8, :], True, True, [("Ust", bq, gh), ("W1", d)], [("ps", b)])
                iv = k.ps[b][:].rearrange("p (ri c) -> p ri c", ri=2)
                if d == 0:
                    ov = S_all[:, 0, :, g32, :]
                else:
                    base = S_all[:, 1, 0, g32, 255:256]
                    ov = bass.AP(base.tensor, base.offset, [list(base.ap[0]), [32 * 256, 2], [-1, 256]])
                k.copy(k.evac_eng(), ov, iv, [("ps", b)], [("S", g32 % 4)])
    P.barrier()

    if S5STOP <= 2:
        return
    Hist = [k.A(65536 + d * 32896, [128, 2, 32, 257], BF16) for d in range(2)]
    Z = [k.A(131328 + i * 512, [128, 2, 2, 32], F32) for i in range(2)]
    T1 = k.A(131328 + 1024, [128, 2, 2, 32], F32)
    T2 = k.A(131328 + 1536, [128, 2, 2, 32], F32)
    for i in range(2):
        P.op("dve", lambda e, i=i: e.memset(Z[i], 0.0), writes=[("Z", i)])
    for d in range(2):
        colz = 0 if d == 0 else 256
        P.op("dve", lambda e, d=d, colz=colz: e.memset(Hist[d][:, :, :, colz], 0.0), writes=[("Hist", d, "z")])
    skeys = [("S", i) for i in range(4)]

    def swapped(z):
        base = z[:, :, 1:2, :]
        return bass.AP(base.tensor, base.offset, [list(base.ap[0]), list(base.ap[1]), [-32, 2], [1, 32]])

    for s_ in range(256):
        zi, zo = s_ % 2, (s_ + 1) % 2
        tt("dve", T1, A1_all, Z[zi], ALU.mult, [("Z", zi), ("ACO", 0, 0), ("ACO", 1, 0), "T1"], ["T1"])
        tt("dve", T2, A2_all, swapped(Z[zi]), ALU.mult, [("Z", zi), ("ACO", 0, 1), ("ACO", 1, 1), "T2"], ["T2"])
        tt("dve", T1, T1, T2, ALU.add, ["T1", "T2"], ["T1"])
        tt("dve", Z[zo], T1, S_all[:, :, :, :, s_], ALU.add, ["T1", ("Z", zo)] + skeys, [("Z", zo)])
        for d in range(2):
            col = s_ + 1 if d == 0 else 255 - s_
            P.op("act", lambda e, d=d, zo=zo, col=col: e.copy(Hist[d][:, :, :, col], Z[zo][:, d, :, :]),
                 reads=[("Z", zo)], writes=[("Hist", d, s_ % 8)])
    P.barrier()

    if S5STOP <= 3:
        return
    Ust = [k.A(0 + i * 4096, [128, 8, 256], BF16) for i in range(2)]
    Tst = [k.A(8192 + i * 2048, [128, 8, 128], BF16) for i in range(2)]
    Vst = [k.A(12288 + i * 8192, [128, 2, 2, 8, 128], BF16) for i in range(2)]
    Yst = [k.A(28672 + i * 2048, [128, 4, 256], BF16) for i in range(2)]
    wglu = k.A(32768, [128, KC, 2048], BF16)
    k.dma("pool", wglu, dr["s5_w_glu"][j].rearrange("(kk p) n -> p kk n", p=128), [], ["wglu"])
    for gb in range(8):
        bq = gb % 2
        gh = gb // 4
        hs = slice(gh * 64, (gh + 1) * 64)
        k.dma("sp", Ust[bq], Dh.ap()[gb * 8:(gb + 1) * 8].rearrange("g a c -> a g c"), [], [("Ust", bq)])
        k.dma("act", Tst[bq], Td.ap()[gb * 8:(gb + 1) * 8].rearrange("g a b -> a g b"), [], [("Tst", bq)])
        for d in range(2):
            for ri in range(2):
                k.dma("sp" if d == 0 else "act", Vst[bq][hs, d, ri],
                      Vd.ap()[d, gb * 8:(gb + 1) * 8].rearrange("g (ri n) q -> n ri g q", ri=2)[:, ri],
                      [], [("Vst", bq, d, ri)])
        for g2 in range(4):
            b = k.nb()
            for gi in range(2):
                g8 = g2 * 2 + gi
                g32 = (gb * 8 + g8) % 32
                o = k.ps[b][:, gi * 256:(gi + 1) * 256]
                k.mm(o, Tst[bq][:, g8, :], Ust[bq][:, g8, :], True, False, [("Tst", bq), ("Ust", bq)], [("ps", b)])
                for d in range(2):
                    c0 = 0 if d == 0 else 1
                    for ri in range(2):
                        k.mm(o, Vst[bq][hs, d, ri, g8, :], Hist[d][hs, ri, g32, c0:c0 + 256], False,
                             (d == 1 and ri == 1), [("Vst", bq, d, ri)], [("ps", b)])
            yb = g2 // 2
            k.copy(k.evac_eng(), Yst[yb][:, (g2 % 2) * 2:(g2 % 2) * 2 + 2, :], k.ps[b][:].rearrange("p (g c) -> p g c", g=2),
                   [("ps", b)], [("Yst", yb)])
            if g2 % 2 == 1:
                g0 = gb * 8 + yb * 4
                k.dma("sp", D2.ap()[g0:g0 + 4].rearrange("g a c -> a g c"), Yst[yb], [("Yst", yb)], [("D2", gb, yb)])
    P.barrier()

    if S5STOP <= 4:
        return
    yTp = k.A(0, [128, KC, 8, 256], BF16)
    gT = k.A(65536, [128, KC, S], BF16)
    sq = k.A(98304, [128, KC, TB], BF16)
    tmpu = k.A(106496, [128, 512], F32)
    tmpw = k.A(110592, [128, 512], F32)
    gd = k.A(118784, [128, 8], F32)
    dsb = k.A(118816, [128, 8], F32)
    d8 = k.A(118848, [128, 128], F32)
    oneb = k.A(119360, [128, 1], F32)
    P.op("dve", lambda e: e.memset(oneb, 1.0), writes=["oneb"])
    k.dma("sp", d8[0:8, :], dr["s5_d"][j].rearrange("(c p) -> c p", p=128), [], ["d8"])
    b = k.nb()
    k.tr(k.ps[b][:, 0:8], d8[0:8, :], k.ident[0:8, 0:8], ["d8", "ident"], [("ps", b)])
    k.copy("dve", dsb, k.ps[b][:, 0:8], [("ps", b)], ["dsb"])
    tt("dve", gd, dsb, k.gsb[:, l * 8:l * 8 + 8], ALU.mult, ["dsb", "gsb"], ["gd"])
    for kc in range(KC):
        for g8 in range(8):
            g = kc * 8 + g8
            k.dma("sp" if g8 % 2 == 0 else "act", yTp[g8 * 16:(g8 + 1) * 16, kc, :, :],
                  D2.ap()[g].rearrange("(i p) c -> p i c", p=16), [], [("yTp", kc)])

    tmps = [[k.A(106496 + (i * 3 + w) * 2048, [128, 512], F32) for w in range(3)] for i in range(2)]

    def after_rstd(tb, ts):
        for kc in range(KC):
            i_ = kc % 2
            tu, tv, tw = tmps[i_]
            ku, kv, kw = ("tmpu", i_), ("tmpv", i_), ("tmpw", i_)
            yv = yTp[:, kc, :, 64 * tb:64 * tb + 64].rearrange("p i c -> p c i")
            v3 = lambda t_: t_.rearrange("p (c i) -> p c i", i=8)
            tt("dve", tu, k.xT[:, kc, ts], k.rstd[:], ALU.mult, [("xT", kc, tb), "rstd", ku], [ku])
            P.op("dve", lambda e, kc=kc, yv=yv, tu=tu: e.scalar_tensor_tensor(v3(tu), v3(tu), gd[:, kc:kc + 1], yv,
                                                                           ALU.mult, ALU.add),
                 reads=[ku, "gd", ("yTp", kc)], writes=[ku])
            P.op("act", lambda e, tu=tu, tv=tv: e.activation(tv, tu, AF.Square), reads=[ku, kv], writes=[kv])
            P.op("act", lambda e, tv=tv: e.activation(tv, tv, AF.Identity, bias=oneb, scale=0.044715), reads=[kv, "oneb"],
                 writes=[kv])
            tt("dve", tv, tv, tu, ALU.mult, [kv, ku], [kv])
            P.op("act", lambda e, tv=tv, tw=tw: e.activation(tw, tv, AF.Sigmoid, scale=1.5957691216057308), reads=[kv, kw],
                 writes=[kw])
            tt("dve", gT[:, kc, ts], tu, tw, ALU.mult, [ku, kw], [("gT", tb)])


    def glu_tb(tb, ts):
        for dcn in range(KC):
            bv = k.nb()
            for c in range(KC):
                k.mm(k.ps[bv][:], wglu[:, c, dcn * 128:(dcn + 1) * 128], gT[:, c, ts], c == 0, c == KC - 1,
                     ["wglu", ("gT", tb)], [("ps", bv)])
            bg = k.nb()
            for c in range(KC):
                k.mm(k.ps[bg][:], wglu[:, c, 1024 + dcn * 128:1024 + (dcn + 1) * 128], gT[:, c, ts], c == 0, c == KC - 1,
                     ["wglu", ("gT", tb)], [("ps", bg)])
            gi_ = dcn % 2
            gw = k.A(119424 + gi_ * 4096, [128, 512], F32)
            gu = k.A(119424 + gi_ * 4096 + 2048, [128, 512], F32)
            P.op("act", lambda e, bg=bg, gw=gw: e.activation(gw, k.ps[bg][:], AF.Sigmoid), reads=[("ps", bg), ("gw", gi_)],
                 writes=[("gw", gi_)])
            tt("dve", gu, k.ps[bv][:], gw, ALU.mult, [("ps", bv), ("gw", gi_), ("gu", gi_)], [("gu", gi_)])
            tt("dve", k.xT[:, dcn, ts], k.xT[:, dcn, ts], gu, ALU.add, [("gu", gi_), ("xT", dcn, tb)], [("xT", dcn, tb)])

    pending = []

    def cb(tb, ts):
        after_rstd(tb, ts)
        if pending:
            glu_tb(*pending.pop(0))
        pending.append((tb, ts))

    rmsnorm(k, l * 8, None, None, sq, only_rstd=cb)
    while pending:
        glu_tb(*pending.pop(0))
    P.barrier()


def build(plan):
    nc = bass.Bass("TRN2", target_bir_lowering=False)
    dr = {}
    dr["x"] = nc.dram_tensor("x", [S, D], F32, kind="ExternalInput").ap()
    for name, shp in PARAM_SPECS:
        dr[name] = nc.dram_tensor(name, shp, F32, kind="ExternalInput").ap()
    for name, shp, dt in CONST_SPECS:
        dr[name] = nc.dram_tensor(name, shp, dt, kind="ExternalInput").ap()
    dr["out"] = nc.dram_tensor("out", [S, D], F32, kind="ExternalOutput").ap()
    with contextlib.ExitStack() as st:
        k = K(nc, st)
        k.xT = k.sb("xT", [128, KC, S], F32)
        k.arena = k.sb("arena", [128, ARENA_BYTES // 2], BF16)
        k.dftc = k.sb("dftc", [128, 256], BF16)
        k.ps = [st.enter_context(nc.psum_tensor("ps%d" % i, [128, 512], F32)) for i in range(8)]
        k.attn_scr = nc.dram_tensor("attn_scr", [16, 3, 384], BF16, kind="Internal")
        k.s5_dh = nc.dram_tensor("s5_dh", [64, 128, 256], BF16, kind="Internal")
        k.s5_d2 = nc.dram_tensor("s5_d2", [64, 128, 256], BF16, kind="Internal")
        k.s5_vd = nc.dram_tensor("s5_vd", [2, 64, 128, 128], BF16, kind="Internal")
        k.s5_td = nc.dram_tensor("s5_td", [64, 128, 128], BF16, kind="Internal")
        k.dma("sp", k.dftc[:], dr["c_dftc"], [], ["dftc"])
        ph_setup(k, dr)
        ph_load_x(k, dr)
        k.P.barrier()
        stored = False
        for pi_, item in enumerate(plan):
            kind, l, j = item
            if kind == "fnet":
                ph_fnet(k, dr, l, j)
            elif kind == "ffn":
                after_fnet = (pi_ > 0 and plan[pi_ - 1][0] == "fnet" and plan[pi_ - 1][1] == l)
                last = (pi_ == len(plan) - 1)
                ph_ffn(k, dr, l, first_slot=1 if after_fnet else 0, preloaded=after_fnet, store=last)
                stored = stored or last
            elif kind == "attn":
                ph_attn(k, dr, l, j)
            elif kind == "s5":
                ph_s5(k, dr, l, j)
            else:
                raise ValueError(kind)
        if not stored:
            ph_store(k, dr)
        k.P.emit(nc)
    return nc


FULL_PLAN = [("fnet", 0, 0), ("ffn", 0, 0), ("s5", 1, 0), ("ffn", 1, 0),
             ("attn", 2, 0), ("ffn", 2, 0), ("fnet", 3, 1), ("ffn", 3, 0)]

_CACHE = {}


def run(inputs, plan):
    key = tuple(plan)
    if key not in _CACHE:
        _CACHE[key] = (build(plan), host_consts())
    nc, consts = _CACHE[key]
    x = np.ascontiguousarray(np.asarray(inputs["x"], dtype=np.float32))
    n = x.shape[0]
    in_maps = []
    for i in range(n):
        m = {"x": x[i]}
        for name, shp in PARAM_SPECS:
            m[name] = np.ascontiguousarray(np.asarray(inputs[name], dtype=np.float32))
        m.update(consts)
        in_maps.append(m)
    res = run_bass_kernel_spmd(nc, in_maps, core_ids=list(range(n)))
    return np.stack([r["out"] for r in res.results], axis=0)


def kernel(**inputs):
    return run(inputs, FULL_PLAN)
```
